# Optimizing a Trainium2 kernel written in Bass

```python
import math
import jax
import jax.numpy as jnp
from jax import lax
import numpy as np

D_MODEL = 4096
BATCH = 4
SEQ = 4096
DEPTH = 1

N_META = 16
CHUNK = 128
RET_HEADS = D_MODEL // 512
RET_DK = 256
RET_DV = 256
SB_HEADS = D_MODEL // 256
SB_DH = 128
RET_WIDTH = RET_HEADS * RET_DV
RET_QK_WIDTH = RET_HEADS * RET_DK
SB_WIDTH = SB_HEADS * SB_DH
MIX_WIDTH = RET_WIDTH + SB_WIDTH
IN_COLS = 2 * RET_QK_WIDTH + 2 * RET_WIDTH + 3 * SB_WIDTH
D_FF = ((8 * D_MODEL // 3 + 255) // 256) * 256
CONV_W = 3
EPS = 1e-6
ROPE_BASE = 10000.0

kernel_name = "hymba_retention_stickbreaking_convffn"


def rms_norm(x, w):
    xf = x.astype(jnp.float32)
    y = xf * lax.rsqrt(jnp.mean(xf * xf, axis=-1, keepdims=True) + EPS)
    return (y * w.astype(jnp.float32)).astype(x.dtype)


def head_rms_norm(y, w):
    b, t, h, d = y.shape
    yf = y.astype(jnp.float32)
    yf = yf * lax.rsqrt(jnp.mean(yf * yf, axis=-1, keepdims=True) + EPS)
    yf = yf * w.astype(jnp.float32).reshape(h, d)
    return yf.reshape(b, t, h * d)


def apply_rotary(x, pos):
    half = x.shape[-1] // 2
    inv = ROPE_BASE ** (-jnp.arange(half, dtype=jnp.float32) / half)
    ang = pos[:, None] * inv[None, :]
    cos = jnp.cos(ang)[None, :, None, :].astype(x.dtype)
    sin = jnp.sin(ang)[None, :, None, :].astype(x.dtype)
    x1, x2 = x[..., :half], x[..., half:]
    return jnp.concatenate([x1 * cos - x2 * sin, x1 * sin + x2 * cos], axis=-1)


def retention_chunkwise(q, k, v):
    b, tp, h, dk = q.shape
    dv = v.shape[-1]
    nc = tp // CHUNK
    log_gamma = jnp.log(1.0 - 2.0 ** (-5.0 - jnp.arange(h, dtype=jnp.float32)))
    idx = jnp.arange(CHUNK, dtype=jnp.float32)
    diff = idx[:, None] - idx[None, :]
    intra_decay = jnp.where(diff[None] >= 0, jnp.exp(jnp.maximum(diff, 0.0)[None] * log_gamma[:, None, None]), 0.0)
    q_decay = jnp.exp((idx[None, :] + 1.0) * log_gamma[:, None])
    k_decay = jnp.exp((CHUNK - 1.0 - idx[None, :]) * log_gamma[:, None])
    chunk_decay = jnp.exp(CHUNK * log_gamma)

    def to_chunks(a):
        return a.reshape(b, nc, CHUNK, h, a.shape[-1]).transpose(1, 0, 3, 2, 4)

    qc, kc, vc = to_chunks(q), to_chunks(k), to_chunks(v)

    def body(state, inp):
        qi, ki, vi = inp
        s = jnp.einsum('bhnd,bhmd->bhnm', qi, ki) * intra_decay[None]
        intra = jnp.einsum('bhnm,bhme->bhne', s, vi)
        cross = jnp.einsum('bhnd,bhde->bhne', qi, state) * q_decay[None, :, :, None]
        new_state = state * chunk_decay[None, :, None, None] + jnp.einsum(
            'bhmd,bhme->bhde', ki * k_decay[None, :, :, None], vi)
        return new_state, intra + cross

    state0 = jnp.zeros((b, h, dk, dv), jnp.float32)
    _, ys = lax.scan(body, state0, (qc, kc, vc))
    return ys.transpose(1, 0, 3, 2, 4).reshape(b, tp, h, dv)


def stick_breaking_attention(q, k, v, key_valid):
    b, tp, h, d = q.shape
    nb = tp // CHUNK
    scale = 1.0 / math.sqrt(d)
    qh = q.transpose(0, 2, 1, 3)
    kh = k.transpose(0, 2, 1, 3)
    vh = v.transpose(0, 2, 1, 3)
    qb = qh.reshape(b, h, nb, CHUNK, d).transpose(2, 0, 1, 3, 4)
    starts = jnp.arange(nb, dtype=jnp.int32) * CHUNK
    key_pos = jnp.arange(tp, dtype=jnp.int32)

    def block(args):
        qblk, start = args
        z = jnp.einsum('bhqd,bhkd->bhqk', qblk, kh).astype(jnp.float32) * scale
        qpos = start + jnp.arange(CHUNK, dtype=jnp.int32)
        mask = (key_pos[None, :] < qpos[:, None]) & key_valid[None, :]
        log_beta = jax.nn.log_sigmoid(z)
        log_keep = jnp.where(mask, jax.nn.log_sigmoid(-z), 0.0)
        rev = lax.cumsum(log_keep, axis=3, reverse=True)
        after = jnp.concatenate([rev[..., 1:], jnp.zeros_like(rev[..., :1])], axis=-1)
        w = jnp.where(mask, jnp.exp(log_beta + after), 0.0)
        return jnp.einsum('bhqk,bhkd->bhqd', w.astype(vh.dtype), vh)

    out = lax.map(block, (qb, starts))
    return out.transpose(1, 0, 3, 2, 4).reshape(b, tp, h, d)


def hybrid_mixer(h, w_in, ret_gn_w, sb_norm_w, w_out):
    b, t, _ = h.shape
    n_pad = CHUNK - N_META
    tp = t + n_pad
    proj = h @ w_in
    proj = jnp.pad(proj, ((0, 0), (n_pad, 0), (0, 0)))
    valid = jnp.arange(tp) >= n_pad
    vmask = valid.astype(proj.dtype)[None, :, None, None]
    pos = (jnp.arange(tp) - n_pad).astype(jnp.float32)

    o = 0
    rq = proj[..., o:o + RET_QK_WIDTH]; o += RET_QK_WIDTH
    rk = proj[..., o:o + RET_QK_WIDTH]; o += RET_QK_WIDTH
    rv = proj[..., o:o + RET_WIDTH]; o += RET_WIDTH
    rg = proj[..., o:o + RET_WIDTH]; o += RET_WIDTH
    sq = proj[..., o:o + SB_WIDTH]; o += SB_WIDTH
    sk = proj[..., o:o + SB_WIDTH]; o += SB_WIDTH
    sv = proj[..., o:o + SB_WIDTH]

    rq = apply_rotary(rq.reshape(b, tp, RET_HEADS, RET_DK), pos) * (RET_DK ** -0.5)
    rk = apply_rotary(rk.reshape(b, tp, RET_HEADS, RET_DK), pos) * vmask
    rv = rv.reshape(b, tp, RET_HEADS, RET_DV) * vmask
    ry = retention_chunkwise(rq, rk, rv)
    ret_out = jax.nn.silu(rg.astype(jnp.float32)) * head_rms_norm(ry, ret_gn_w)

    sq = sq.reshape(b, tp, SB_HEADS, SB_DH)
    sk = sk.reshape(b, tp, SB_HEADS, SB_DH) * vmask
    sv = sv.reshape(b, tp, SB_HEADS, SB_DH) * vmask
    sy = stick_breaking_attention(sq, sk, sv, valid)
    sb_out = head_rms_norm(sy, sb_norm_w)

    mixed = jnp.concatenate([ret_out, sb_out], axis=-1)[:, n_pad:].astype(h.dtype)
    return mixed @ w_out


def conv_ffn(h, w_up, conv_w, conv_b, w_down):
    t = h.shape[1]
    gu = h @ w_up
    gate, up = gu[..., :D_FF], gu[..., D_FF:]
    gp = jnp.pad(gate, ((0, 0), (CONV_W - 1, 0), (0, 0)))
    conv = conv_b[None, None, :]
    for i in range(CONV_W):
        conv = conv + gp[:, i:i + t] * conv_w[i][None, None, :]
    return (jax.nn.silu(conv) * up) @ w_down


def setup_inputs(seed: int = 0) -> dict:
    key = jax.random.key(seed)
    ks = jax.random.split(key, 16)
    f32 = jnp.float32

    def gain(k, n):
        return 1.0 + 0.01 * jax.random.normal(k, (DEPTH, n), f32)

    return {
        "x": jax.random.normal(ks[0], (BATCH, SEQ, D_MODEL), f32),
        "meta_tokens": jax.random.normal(ks[1], (N_META, D_MODEL), f32),
        "attn_pre_norm_w": gain(ks[2], D_MODEL),
        "w_in": jax.random.normal(ks[3], (DEPTH, D_MODEL, IN_COLS), f32) * D_MODEL ** -0.5,
        "ret_gn_w": gain(ks[4], RET_WIDTH),
        "sb_norm_w": gain(ks[5], SB_WIDTH),
        "w_out": jax.random.normal(ks[6], (DEPTH, MIX_WIDTH, D_MODEL), f32) * MIX_WIDTH ** -0.5,
        "attn_post_norm_w": gain(ks[7], D_MODEL),
        "ffn_pre_norm_w": gain(ks[8], D_MODEL),
        "w_up": jax.random.normal(ks[9], (DEPTH, D_MODEL, 2 * D_FF), f32) * D_MODEL ** -0.5,
        "conv_w": jax.random.normal(ks[10], (DEPTH, CONV_W, D_FF), f32) * CONV_W ** -0.5,
        "conv_b": 0.01 * jax.random.normal(ks[11], (DEPTH, D_FF), f32),
        "w_down": jax.random.normal(ks[12], (DEPTH, D_FF, D_MODEL), f32) * D_FF ** -0.5,
        "ffn_post_norm_w": gain(ks[13], D_MODEL),
    }


def reference(x, meta_tokens, attn_pre_norm_w, w_in, ret_gn_w, sb_norm_w, w_out,
              attn_post_norm_w, ffn_pre_norm_w, w_up, conv_w, conv_b, w_down, ffn_post_norm_w):
    b = x.shape[0]
    meta = jnp.broadcast_to(meta_tokens[None].astype(x.dtype), (b, N_META, x.shape[-1]))
    h = jnp.concatenate([meta, x], axis=1)
    for l in range(DEPTH):
        a = hybrid_mixer(rms_norm(h, attn_pre_norm_w[l]), w_in[l], ret_gn_w[l], sb_norm_w[l], w_out[l])
        h = h + rms_norm(a, attn_post_norm_w[l])
        f = conv_ffn(rms_norm(h, ffn_pre_norm_w[l]), w_up[l], conv_w[l], conv_b[l], w_down[l])
        h = h + rms_norm(f, ffn_post_norm_w[l])
    return h[:, N_META:]
```

```python
import contextlib
import math
import os
import numpy as np
import concourse.bass as bass
import concourse.mybir as mybir
from concourse.bass_utils import run_bass_kernel_spmd

F32 = mybir.dt.float32
BF16 = mybir.dt.bfloat16
AF = mybir.ActivationFunctionType
ALU = mybir.AluOpType

SEM_LIMIT = 30000
STRICT_SAME_ENGINE = True

D = 4096
DC = 32
BATCH = 4
SEQ = 4096
N_META = 16
RH, SH = 8, 16
IN_COLS = 14336
D_FF = 11008
FB = 86
EPS = 1e-6
NOWN, NKV, NL = 17, 16, 33
TOWN, TKV, TL = NOWN * 128, NKV * 128, NL * 128
NOUT = 16 * 128


class Tok:
    __slots__ = ("eng", "sem", "val", "owner")

    def __init__(self, eng, sem=None, val=None, owner=None):
        self.eng = eng
        self.sem = sem
        self.val = val
        self.owner = owner


class T:
    __slots__ = ("name", "w", "r", "dsem", "dcnt")

    def __init__(self, name=""):
        self.name = name
        self.w = None
        self.r = {}
        self.dsem = None
        self.dcnt = 0


class Sched:
    ENGS = ("pe", "act", "dve", "pool", "sp")

    def __init__(self, nc, stack):
        self.nc = nc
        self.stack = stack
        self.q = {e: [] for e in self.ENGS}
        self.cnt = {e: 0 for e in self.ENGS}
        self.sem = {e: None for e in self.ENGS}
        self.known = {e: {} for e in self.ENGS}
        self.pending_pe = []
        self.owners = []
        self.nsem = 0
        self.n_wait = 0
        self.n_ins = 0

    def new_sem(self, name):
        self.nsem += 1
        return self.stack.enter_context(self.nc.semaphore(f"{name}_{self.nsem}"))

    def _eng_sem(self, e):
        if self.sem[e] is None or self.cnt[e] >= SEM_LIMIT:
            self.sem[e] = self.new_sem("e" + e)
            self.cnt[e] = 0
        return self.sem[e]

    def _waits(self, e, reads, writes):
        need = {}

        def add(tok, kind):
            if tok is None:
                return
            if tok.eng == e:
                if e == "pe" or e == "sp":
                    return
                if not STRICT_SAME_ENGINE or kind == "war":
                    return
            if tok.eng == "dma":
                val = tok.owner.dcnt * 16
            else:
                val = tok.val
                if val is None:
                    raise RuntimeError("dependency on unresolved PE token")
            k = id(tok.sem)
            if k not in need or need[k][1] < val:
                need[k] = (tok.sem, val)

        for t in reads:
            add(t.w, "raw")
        for t in writes:
            add(t.w, "waw")
            for tok in t.r.values():
                add(tok, "war")
        return self._filter(e, need)

    def _filter(self, e, need):
        out = []
        kn = self.known[e]
        for k, (sem, val) in need.items():
            if kn.get(k, 0) >= val:
                continue
            kn[k] = val
            out.append((sem, val))
        return out

    def _record(self, tok, key, reads, writes):
        for t in writes:
            t.w = tok
            t.r = {}
        for t in reads:
            t.r[key] = tok

    def op(self, e, fn, reads=(), writes=(), inc=True):
        for (sem, val) in self._waits(e, reads, writes):
            self.q[e].append(("wait", sem, val))
            self.n_wait += 1
        self.n_ins += 1
        if inc:
            sem = self._eng_sem(e)
            self.cnt[e] += 1
            tok = Tok(e, sem, self.cnt[e])
            self.q[e].append(("ins", fn, sem, 1))
            if e == "pe":
                for p in self.pending_pe:
                    p.sem, p.val = sem, self.cnt[e]
                self.pending_pe = []
        else:
            assert e == "pe"
            tok = Tok(e, None, None)
            self.pending_pe.append(tok)
            self.q[e].append(("ins", fn, None, 0))
        self._record(tok, e, reads, writes)
        return tok

    def dma(self, e, out, in_, reads, writes, owner, **kw):
        for (sem, val) in self._waits(e, reads, writes):
            self.q[e].append(("wait", sem, val))
            self.n_wait += 1
        if owner.dsem is None:
            owner.dsem = self.new_sem("d")
            self.owners.append(owner)
        owner.dcnt += 1
        tok = Tok("dma", owner.dsem, owner.dcnt * 16, owner)
        self.q[e].append(("ins", lambda h, o=out, i=in_, k=kw: h.dma_start(out=o, in_=i, **k),
                          owner.dsem, 16))
        self.n_ins += 1
        self._record(tok, ("dma", id(owner.dsem)), reads, writes)
        return tok

    def barrier(self):
        assert not self.pending_pe
        allw = {}
        for e in self.ENGS:
            if self.sem[e] is not None and self.cnt[e] > 0:
                allw[id(self.sem[e])] = (self.sem[e], self.cnt[e], e)
        for o in self.owners:
            if o.dcnt:
                allw[id(o.dsem)] = (o.dsem, o.dcnt * 16, "dma")
        for e in self.ENGS:
            need = {k: (s, v) for k, (s, v, src) in allw.items() if src != e}
            for (sem, val) in self._filter(e, need):
                self.q[e].append(("wait", sem, val))
                self.n_wait += 1

    def replay(self, e, h):
        for item in self.q[e]:
            if item[0] == "wait":
                h.wait_ge(item[1], item[2])
            else:
                ins = item[1](h)
                if item[2] is not None:
                    ins.then_inc(item[2], item[3])

    def run_block(self):
        with self.nc.Block() as block:
            @block.tensor
            def _(h):
                self.replay("pe", h)

            @block.scalar
            def _(h):
                self.replay("act", h)

            @block.vector
            def _(h):
                self.replay("dve", h)

            @block.gpsimd
            def _(h):
                self.replay("pool", h)

            @block.sync
            def _(h):
                self.replay("sp", h)


class Arena:
    def __init__(self, big, nbytes):
        self.big = big
        self.nbytes = nbytes
        self.off = 0

    def reset(self):
        self.off = 0

    def alloc(self, shape, dt):
        size = 4 if dt == F32 else 2
        n = 1
        for s in shape[1:]:
            n *= s
        nb = (n * size + 63) // 64 * 64
        assert self.off + nb <= self.nbytes, f"SBUF arena overflow {self.off + nb} > {self.nbytes}"
        v = self.big[:, self.off // 2: self.off // 2 + n * size // 2]
        self.off += nb
        if dt == F32:
            v = v.bitcast(F32)
        if len(shape) == 3:
            v = v.rearrange("p (a b) -> p a b", a=shape[1])
        elif len(shape) == 4:
            v = v.rearrange("p (a b c) -> p a b c", a=shape[1], b=shape[2])
        return v


def build_program(stop_after=99, debug=False):
    nc = bass.Bass("TRN2", target_bir_lowering=False)
    okind = "ExternalOutput" if debug else "Internal"

    def din(name, shape, dt=F32):
        return nc.dram_tensor(name, list(shape), dt, kind="ExternalInput").ap()

    def dscr(name, shape, dt):
        return nc.dram_tensor(name, list(shape), dt, kind=okind).ap()

    xown = din("xown", [TOWN, D])
    xkv = din("xkv", [TKV, D])
    cs_own = din("cs_own", [2, 128, TOWN])
    cs_kv = din("cs_kv", [2, 128, TKV])
    kval_d = din("kval", [128, NL])
    ident_d = din("ident", [128, 128])
    negtri_d = din("negtri", [128, 128])
    cmask_d = din("cmask", [128, 128])
    gq_d = din("gq", [128, RH * 128])
    dm_d = din("dm", [128, RH * 128])
    kdec_d = din("kdec", [128, RH])
    wpre_d = din("wpre", [128, DC])
    wffnpre_d = din("wffnpre", [128, DC])
    wpost_d = din("wpost", [1, D])
    wffnpost_d = din("wffnpost", [1, D])
    gnw_d = din("gnw", [1, 2048])
    sbw_d = din("sbw", [128, SH])
    cw_d = din("cw", [128, FB * 3])
    cb_d = din("cb", [128, FB])
    win_d = din("win", [112, 128, DC * 128])
    wout_d = din("wout", [8, 128, DC * 512])
    wup_d = din("wup", [2 * FB, 128, DC * 128])
    wdn_d = din("wdn", [8, 128, FB * 512])
    out_d = nc.dram_tensor("out", [NOUT, D], F32, kind="ExternalOutput").ap()

    rqT = dscr("rqT", [RH, 2, 128, TOWN], BF16)
    rkT = dscr("rkT", [RH, 2, 128, TL], BF16)
    rv = dscr("rv", [TL, 2048], BF16)
    rgs = dscr("rgs", [TOWN, 2048], BF16)
    sqT = dscr("sqT", [SH, 128, TOWN], BF16)
    skT = dscr("skT", [SH, 128, TL], BF16)
    sv = dscr("sv", [TL, 2048], BF16)
    mixT = dscr("mixT", [DC, 128, TOWN], BF16)
    a_raw = dscr("a_raw", [TOWN, D], F32)
    h1_d = dscr("h1", [TOWN, D], F32)
    h2T = dscr("h2T", [DC, 128, TOWN], BF16)
    f_raw = dscr("f_raw", [NOUT, D], F32)

    gam = [1.0 - 2.0 ** (-5.0 - h) for h in range(RH)]

    with contextlib.ExitStack() as st:
        S = Sched(nc, st)
        ARENA_BYTES = 198 * 1024
        big = st.enter_context(nc.sbuf_tensor("big", [128, ARENA_BYTES // 2], BF16))
        ps = st.enter_context(nc.psum_tensor("ps", [128, 8, 512], F32))
        AR = Arena(big, ARENA_BYTES)
        Tps = [T(f"ps{i}") for i in range(8)]
        Tin = T("in")

        def psb(b):
            return ps[:, b, :].bitcast(BF16)

        Tg = {}

        def TD(name, i=0):
            k = (name, i)
            if k not in Tg:
                Tg[k] = T(name)
            return Tg[k]

        def new_phase():
            S.barrier()
            AR.reset()

        def mm(out, lhsT, rhs, start, stop, reads, writes, inc, **kw):
            S.op("pe", lambda h, o=out, l=lhsT, r=rhs, s=start, e=stop, k=kw:
                 h.matmul(o, lhsT=l, rhs=r, start=s, stop=e, **k), reads, writes, inc=inc)

        def tr(out, in_, ident, reads, writes, inc):
            S.op("pe", lambda h, o=out, i=in_, d=ident: h.transpose(out=o, in_=i, identity=d),
                 reads, writes, inc=inc)

        def act(out, in_, func, reads, writes, **kw):
            S.op("act", lambda h, o=out, i=in_, f=func, k=kw: h.activation(out=o, in_=i, func=f, **k),
                 reads, writes)

        def tt(eng, out, in0, in1, op, reads, writes):
            S.op(eng, lambda h, o=out, a=in0, b=in1, p=op: h.tensor_tensor(out=o, in0=a, in1=b, op=p),
                 reads, writes)

        def tsc(eng, out, in0, s1, op0, reads, writes, s2=None, op1=None):
            if op1 is None:
                S.op(eng, lambda h, o=out, a=in0, s=s1, p=op0: h.tensor_scalar(out=o, in0=a, scalar1=s, scalar2=None, op0=p),
                     reads, writes)
            else:
                S.op(eng, lambda h, o=out, a=in0, s=s1, z=s2, p=op0, q=op1:
                     h.tensor_scalar(out=o, in0=a, scalar1=s, scalar2=z, op0=p, op1=q), reads, writes)

        def stt(eng, out, in0, scalar, in1, op0, op1, reads, writes):
            S.op(eng, lambda h, o=out, a=in0, s=scalar, b=in1, p=op0, q=op1:
                 h.scalar_tensor_tensor(out=o, in0=a, scalar=s, in1=b, op0=p, op1=q), reads, writes)

        def cp(eng, out, in_, reads, writes):
            if eng == "act":
                S.op(eng, lambda h, o=out, i=in_: h.activation(out=o, in_=i, func=AF.Copy), reads, writes)
            else:
                S.op(eng, lambda h, o=out, i=in_: h.tensor_copy(out=o, in_=i), reads, writes)

        def mset(eng, ap, val, writes):
            S.op(eng, lambda h, a=ap, v=val: h.memset(a, v), [], writes)

        def recip(out, in_, reads, writes):
            S.op("dve", lambda h, o=out, i=in_: h.reciprocal(out=o, in_=i), reads, writes)

        def load(out, in_, reads, writes, owner, q="sp", **kw):
            return S.dma(q, out, in_, reads, writes, owner, **kw)

        def load_split(out, in_, reads, writes, owner, rows=4, q="sp"):
            R = out.shape[1]
            tok = None
            for r0 in range(0, R, rows):
                r1 = min(R, r0 + rows)
                tok = S.dma(q, out[:, r0:r1], in_[:, r0:r1], reads, writes, owner)
            return tok

        def wload(out, in_, tslot):
            return S.dma("pool", out, in_, [Tin], [tslot], tslot, max_dma_last_dim=8192)

        def rstd_from_ssq(ssq, tmp, out, n, Tst):
            act(tmp, ssq, AF.Sqrt, [Tst], [Tst], scale=1.0 / n, bias=EPS)
            recip(out, tmp, [Tst], [Tst])

        CONST_BYTES = 8 * 1024
        cbig = st.enter_context(nc.sbuf_tensor("cbig", [128, CONST_BYTES // 2], BF16))
        CA = Arena(cbig, CONST_BYTES)
        identf = CA.alloc([128, 128], F32)
        identb = CA.alloc([128, 128], BF16)
        kval = CA.alloc([128, NL], F32)
        wpre = CA.alloc([128, DC], F32)
        wffnpre = CA.alloc([128, DC], F32)
        sbw = CA.alloc([128, SH], F32)
        kdec = CA.alloc([128, RH], F32)
        Tc = T("consts")
        load(identf, ident_d, [Tin], [Tc], Tc)
        load(kval, kval_d, [Tin], [Tc], Tc)
        load(wpre, wpre_d, [Tin], [Tc], Tc)
        load(wffnpre, wffnpre_d, [Tin], [Tc], Tc)
        load(sbw, sbw_d, [Tin], [Tc], Tc)
        load(kdec, kdec_d, [Tin], [Tc], Tc)
        Tidb = T("identb")
        cp("dve", identb, identf, [Tc], [Tidb])

        def norm_transpose(src_rows, xt, xn, st4, Txt, Txn, Tst, wvec, dst, dst_reads, dst_writes,
                           tbanks, h_in=None, h1_out=None):
            mset("dve", st4[:, 0:1], 0.0, [Tst])
            act(xn, xt, AF.Square, [Txt, Tst], [Txn, Tst], accum_out=st4[:, 0:1])
            rstd_from_ssq(st4[:, 0:1], st4[:, 1:2], st4[:, 2:3], D, Tst)
            tsc("dve", xn, xt, st4[:, 2:3], ALU.mult, [Txt, Tst], [Txn])
            for g in range(4):
                b = tbanks[g % 2]
                pv = psb(b)
                for j in range(8):
                    c = g * 8 + j
                    tr(pv[:, j * 128:(j + 1) * 128], xn[:, c * 128:(c + 1) * 128], identb,
                       [Txn, Tidb], [Tps[b]], inc=(j == 7))
                tt("dve", dst(g * 8), pv.rearrange("p (j t) -> p j t", j=8),
                   wvec[:, g * 8:(g + 1) * 8].unsqueeze(2).to_broadcast([128, 8, 128]), ALU.mult,
                   [Tps[b], Tc] + dst_reads, dst_writes)

        hT = AR.alloc([128, DC, 1152], BF16)
        wsl = [AR.alloc([128, DC * 128], BF16) for _ in range(4)]
        Tws = [T(f"ws{i}") for i in range(4)]
        xt = [AR.alloc([128, D], F32) for _ in range(2)]
        xn = [AR.alloc([128, D], BF16) for _ in range(2)]
        Txt = [T("xt0"), T("xt1")]
        Txn = [T("xn0"), T("xn1")]
        cs = AR.alloc([128, 2, 1152], F32)
        Tcs = T("cs")
        st4 = [AR.alloc([128, 4], F32) for _ in range(2)]
        Tst = [T("st0"), T("st1")]
        gq = AR.alloc([128, RH, 128], F32)
        Tgq = T("gq")
        load(gq.rearrange("p a b -> p (a b)"), gq_d, [Tin], [Tgq], Tgq)
        NR = 3
        rt = [[AR.alloc([128, 512], F32) for _ in range(3)] for _ in range(2)]
        Trt = [T("rt0"), T("rt1")]
        ob = [AR.alloc([128, 2, 512], BF16) for _ in range(NR)]
        Tob = [T(f"ob{i}") for i in range(NR)]
        tb = [AR.alloc([128, 512], BF16) for _ in range(2)]
        Ttb = [T("tb0"), T("tb1")]
        ob2 = [AR.alloc([128, 4, 128], BF16) for _ in range(2)]
        Tob2 = [T("o20"), T("o21")]
        ThT = [T(f"hT{i}") for i in range(9)]

        wq = {"n": 0}
        obq = {"n": 0, "t": 0, "r": 0}
        bankq = {"n": 0}

        def in_proj_pass(src, src_c0, nch, own):
            ntok = nch * 128
            loc0 = (NKV + src_c0 if own else src_c0)
            csrc = cs_own if own else cs_kv
            load(cs[:, :, 0:ntok], csrc[:, :, src_c0 * 128: src_c0 * 128 + ntok].rearrange("a p t -> p a t"),
                 [Tin], [Tcs], Tcs)
            for ci in range(nch):
                i = ci % 2
                r0 = (src_c0 + ci) * 128
                load(xt[i], src[r0:r0 + 128, :], [Tin], [Txt[i]], Txt[i])
                norm_transpose(None, xt[i], xn[i], st4[i], Txt[i], Txn[i], Tst[i], wpre,
                               lambda c0, ci=ci: hT[:, c0:c0 + 8, ci * 128:(ci + 1) * 128],
                               [], [ThT[ci]], (6, 7))
            tiles = [(t0, min(512, ntok - t0)) for t0 in range(0, ntok, 512)]
            allhT = [ThT[ci] for ci in range(nch)]

            def wfetch(blk):
                s = wq["n"] % 4
                wq["n"] += 1
                wload(wsl[s], win_d[blk], Tws[s])
                return s

            deferred = []

            def run_deferred():
                while deferred:
                    deferred.pop(0)()

            def gemm(s, t0, n, bank):
                w3 = wsl[s].rearrange("p (k c) -> p k c", k=DC)
                pend_ = list(deferred)
                del deferred[:]
                for kc in range(DC):
                    mm(ps[:, bank, 0:n], w3[:, kc, :], hT[:, kc, t0:t0 + n], kc == 0, kc == DC - 1,
                       [Tws[s]] + allhT, [Tps[bank]], inc=(kc == DC - 1))
                for f_ in pend_:
                    f_()

            def unit_rope(kind, h):
                base = 0 if kind == "q" else 16
                s1 = wfetch(base + 2 * h)
                s2 = wfetch(base + 2 * h + 1)
                for (t0, n) in tiles:
                    b0 = (bankq["n"] % 2) * 2
                    bankq["n"] += 1
                    gemm(s1, t0, n, b0)
                    gemm(s2, t0, n, b0 + 1)
                    ri = obq["r"] % 2
                    obq["r"] += 1
                    oi = obq["n"] % NR
                    obq["n"] += 1
                    r = rt[ri]
                    cosv, sinv = cs[:, 0, t0:t0 + n], cs[:, 1, t0:t0 + n]
                    p1, p2 = ps[:, b0, 0:n], ps[:, b0 + 1, 0:n]
                    nchk = n // 128
                    gb = gq[:, h, :].unsqueeze(1).to_broadcast([128, nchk, 128])
                    tt("dve", r[0][:, 0:n], p1, cosv, ALU.mult, [Tps[b0], Tcs], [Trt[ri]])
                    tt("dve", r[1][:, 0:n], p2, sinv, ALU.mult, [Tps[b0 + 1], Tcs], [Trt[ri]])
                    if kind == "q":
                        tt("dve", r[0][:, 0:n], r[0][:, 0:n], r[1][:, 0:n], ALU.subtract, [Trt[ri]], [Trt[ri]])
                        tt("dve", ob[oi][:, 0, 0:n].rearrange("p (a b) -> p a b", a=nchk),
                           r[0][:, 0:n].rearrange("p (a b) -> p a b", a=nchk), gb, ALU.mult,
                           [Trt[ri], Tgq], [Tob[oi]])
                    else:
                        tt("dve", ob[oi][:, 0, 0:n], r[0][:, 0:n], r[1][:, 0:n], ALU.subtract, [Trt[ri]], [Tob[oi]])
                    tt("dve", r[0][:, 0:n], p1, sinv, ALU.mult, [Tps[b0], Tcs], [Trt[ri]])
                    tt("dve", r[1][:, 0:n], p2, cosv, ALU.mult, [Tps[b0 + 1], Tcs], [Trt[ri]])
                    if kind == "q":
                        tt("dve", r[0][:, 0:n], r[0][:, 0:n], r[1][:, 0:n], ALU.add, [Trt[ri]], [Trt[ri]])
                        tt("dve", ob[oi][:, 1, 0:n].rearrange("p (a b) -> p a b", a=nchk),
                           r[0][:, 0:n].rearrange("p (a b) -> p a b", a=nchk), gb, ALU.mult,
                           [Trt[ri], Tgq], [Tob[oi]])
                        dst = rqT[h, :, :, src_c0 * 128 + t0: src_c0 * 128 + t0 + n]
                        tdst = TD("rqT")
                    else:
                        tt("dve", ob[oi][:, 1, 0:n], r[0][:, 0:n], r[1][:, 0:n], ALU.add, [Trt[ri]], [Tob[oi]])
                        dst = rkT[h, :, :, loc0 * 128 + t0: loc0 * 128 + t0 + n]
                        tdst = TD("rkT")
                    load(dst.rearrange("a p t -> p a t"), ob[oi][:, :, 0:n], [Tob[oi]], [tdst], Tob[oi])

            def unit_fm(kind, h):
                blk = (64 if kind == "sq" else 80) + h
                s = wfetch(blk)
                for (t0, n) in tiles:
                    b0 = bankq["n"] % 4
                    bankq["n"] += 1
                    gemm(s, t0, n, b0)
                    oi = obq["n"] % NR
                    obq["n"] += 1
                    if kind == "sq":
                        act(ob[oi][:, 0, 0:n], ps[:, b0, 0:n], AF.Copy, [Tps[b0]], [Tob[oi]], scale=1.0 / math.sqrt(128.0))
                        dst, tdst = sqT[h, :, src_c0 * 128 + t0: src_c0 * 128 + t0 + n], TD("sqT")
                    else:
                        act(ob[oi][:, 0, 0:n], ps[:, b0, 0:n], AF.Copy, [Tps[b0]], [Tob[oi]])
                        dst, tdst = skT[h, :, loc0 * 128 + t0: loc0 * 128 + t0 + n], TD("skT")
                    load(dst, ob[oi][:, 0, 0:n], [Tob[oi]], [tdst], Tob[oi])

            def unit_tm(kind, cb):
                blk = {"rv": 32, "rg": 48, "sv": 96}[kind] + cb
                s = wfetch(blk)
                for (t0, n) in tiles:
                    b0 = bankq["n"] % 4
                    bankq["n"] += 1
                    gemm(s, t0, n, b0)
                    ti = obq["t"] % 2
                    obq["t"] += 1
                    nchk = n // 128
                    act(tb[ti][:, 0:n], ps[:, b0, 0:n], AF.Silu if kind == "rg" else AF.Copy, [Tps[b0]], [Ttb[ti]])

                    def epi(ti=ti, n=n, nchk=nchk, t0=t0, kind=kind, cb=cb):
                        bt = 4 + ti
                        pv = psb(bt)
                        for j in range(nchk):
                            tr(pv[:, j * 128:(j + 1) * 128], tb[ti][:, j * 128:(j + 1) * 128], identb,
                               [Ttb[ti], Tidb], [Tps[bt]], inc=(j == nchk - 1))
                        pv3 = pv[:, 0:n].rearrange("p (j c) -> p j c", j=nchk)
                        lc0 = loc0 + t0 // 128
                        if kind == "rg":
                            cp("act", ob2[ti][:, 0:nchk, :], pv3, [Tps[bt]], [Tob2[ti]])
                            dst = rgs[(src_c0 * 128 + t0):(src_c0 * 128 + t0 + n), cb * 128:(cb + 1) * 128]
                            tdst = TD("rgs")
                        else:
                            tt("dve", ob2[ti][:, 0:nchk, :], pv3,
                               kval[:, lc0:lc0 + nchk].unsqueeze(2).to_broadcast([128, nchk, 128]), ALU.mult,
                               [Tps[bt], Tc], [Tob2[ti]])
                            dd = rv if kind == "rv" else sv
                            dst = dd[(loc0 * 128 + t0):(loc0 * 128 + t0 + n), cb * 128:(cb + 1) * 128]
                            tdst = TD(kind)
                        load(dst.rearrange("(j p) c -> p j c", p=128), ob2[ti][:, 0:nchk, :], [Tob2[ti]], [tdst], Tob2[ti])

                    deferred.append(epi)

            for h in range(RH):
                if own:
                    unit_rope("q", h)
                unit_rope("k", h)
            for cb in range(16):
                unit_tm("rv", cb)
            if own:
                for cb in range(16):
                    unit_tm("rg", cb)
                for h in range(SH):
                    unit_fm("sq", h)
            for h in range(SH):
                unit_fm("sk", h)
            for cb in range(16):
                unit_tm("sv", cb)
            run_deferred()

        in_proj_pass(xkv, 0, 8, False)
        in_proj_pass(xkv, 8, 8, False)
        in_proj_pass(xown, 0, 9, True)
        in_proj_pass(xown, 9, 8, True)

        if stop_after >= 2:
            new_phase()
            kTc = [AR.alloc([128, 16, 128], BF16) for _ in range(2)]
            vch = [AR.alloc([128, 2048], BF16) for _ in range(2)]
            qTc = [AR.alloc([128, 16, 128], BF16) for _ in range(2)]
            rgc = [AR.alloc([128, 2048], BF16) for _ in range(2)]
            Tk2, Tv2, Tq2, Tg2 = ([T("k0"), T("k1")], [T("v0"), T("v1")], [T("q0"), T("q1")], [T("g0"), T("g1")])
            S32 = AR.alloc([128, 16, 256], F32)
            Sbf = AR.alloc([128, 16, 256], BF16)
            TS32 = [T(f"S32_{h}") for h in range(RH)]
            TSbf = [T(f"Sbf_{h}") for h in range(RH)]
            kd = [AR.alloc([128, 256], BF16) for _ in range(2)]
            Tkd = [T("kd0"), T("kd1")]
            sTm = [AR.alloc([128, 128], BF16) for _ in range(2)]
            TsTm = [T("sT0"), T("sT1")]
            yall = AR.alloc([128, RH, 256], F32)
            Tyall = T("yall")
            junk2 = AR.alloc([128, 256], F32)
            Tjunk2 = T("junk2")
            ssq = AR.alloc([128, 3 * RH], F32)
            Tssq = T("ssq")
            gnw = AR.alloc([128, 2048], F32)
            dmt = AR.alloc([128, RH, 128], F32)
            Tct2 = T("ct2")
            load(gnw, gnw_d.partition_broadcast(128), [Tin], [Tct2], Tct2)
            load(dmt.rearrange("p a b -> p (a b)"), dm_d, [Tin], [Tct2], Tct2)
            ymix = AR.alloc([128, 2048], BF16)
            Tymix = T("ymix")
            mst = [AR.alloc([128, 16, 128], BF16) for _ in range(2)]
            Tmst = [T("mst0"), T("mst1")]
            for h in range(RH):
                mset("dve", S32[:, 2 * h:2 * h + 2, :], 0.0, [TS32[h]])
                mset("dve", Sbf[:, 2 * h:2 * h + 2, :], 0.0, [TSbf[h]])

            def p2_loads(lc):
                i = lc % 2
                load_split(kTc[i], rkT[:, :, :, lc * 128:(lc + 1) * 128].rearrange("h a p t -> p (h a) t"),
                           [TD("rkT")], [Tk2[i]], Tk2[i])
                load(vch[i], rv[lc * 128:(lc + 1) * 128, :], [TD("rv")], [Tv2[i]], Tv2[i])
                if lc >= NKV:
                    m_ = lc - NKV
                    load_split(qTc[i], rqT[:, :, :, m_ * 128:(m_ + 1) * 128].rearrange("h a p t -> p (h a) t"),
                               [TD("rqT")], [Tq2[i]], Tq2[i])
                    load(rgc[i], rgs[m_ * 128:(m_ + 1) * 128, :], [TD("rgs")], [Tg2[i]], Tg2[i])

            def p2_s1(lc, h, u):
                i = lc % 2
                own = lc >= NKV
                last = (lc == NL - 1)
                if not last:
                    pv = psb(u)
                    for half in range(2):
                        tr(pv[:, half * 128:(half + 1) * 128], kTc[i][:, 2 * h + half, :], identb,
                           [Tk2[i], Tidb], [Tps[u]], inc=(half == 1))
                    tsc("dve", kd[u], pv[:, 0:256], kdec[:, h:h + 1], ALU.mult, [Tps[u], Tc], [Tkd[u]])
                if own:
                    bs = 2 + u
                    for half in range(2):
                        mm(ps[:, bs, 0:128], kTc[i][:, 2 * h + half, :], qTc[i][:, 2 * h + half, :],
                           half == 0, half == 1, [Tk2[i], Tq2[i]], [Tps[bs]], inc=(half == 1))
                    tt("dve", sTm[u], ps[:, bs, 0:128], dmt[:, h, :], ALU.mult, [Tps[bs], Tct2], [TsTm[u]])

            def p2_s2(lc, h, u):
                i = lc % 2
                own = lc >= NKV
                last = (lc == NL - 1)
                if own:
                    bo = 4 + u
                    mm(ps[:, bo, 0:256], sTm[u], vch[i][:, h * 256:(h + 1) * 256], True, False,
                       [TsTm[u], Tv2[i]], [Tps[bo]], inc=False)
                    for half in range(2):
                        mm(ps[:, bo, 0:256], qTc[i][:, 2 * h + half, :], Sbf[:, 2 * h + half, :], False, half == 1,
                           [Tq2[i], TSbf[h]], [Tps[bo]], inc=(half == 1))
                    act(yall[:, h, :], ps[:, bo, 0:256], AF.Copy, [Tps[bo]], [Tyall])
                    act(junk2, ps[:, bo, 0:256], AF.Square, [Tps[bo], Tssq], [Tjunk2, Tssq], accum_out=ssq[:, h:h + 1])
                if not last:
                    bu = 6 + u
                    for half in range(2):
                        mm(ps[:, bu, half * 256:(half + 1) * 256], kd[u][:, half * 128:(half + 1) * 128],
                           vch[i][:, h * 256:(h + 1) * 256], True, True, [Tkd[u], Tv2[i]], [Tps[bu]], inc=(half == 1))
                    s32v = S32[:, 2 * h:2 * h + 2, :].rearrange("p a b -> p (a b)")
                    stt("dve", s32v, s32v, float(gam[h] ** 128), ps[:, bu, :], ALU.mult, ALU.add,
                        [Tps[bu], TS32[h]], [TS32[h]])
                    cp("act", Sbf[:, 2 * h:2 * h + 2, :].rearrange("p a b -> p (a b)"), s32v, [TS32[h]], [TSbf[h]])

            ui = 0
            pend = None
            p2_loads(0)
            for lc in range(NL):
                i = lc % 2
                own = lc >= NKV
                m = lc - NKV
                if own:
                    mset("dve", ssq[:, 0:RH], 0.0, [Tssq])
                for h in range(RH):
                    u = ui % 2
                    ui += 1
                    p2_s1(lc, h, u)
                    if pend is not None:
                        p2_s2(*pend)
                    pend = (lc, h, u)
                    if h == 0 and lc + 1 < NL:
                        p2_loads(lc + 1)
                if own:
                    p2_s2(*pend)
                    pend = None
                    rstd_from_ssq(ssq[:, 0:RH], ssq[:, RH:2 * RH], ssq[:, 2 * RH:3 * RH], 256, Tssq)
                    tt("dve", yall, yall, ssq[:, 2 * RH:3 * RH].unsqueeze(2).to_broadcast([128, RH, 256]), ALU.mult,
                       [Tyall, Tssq], [Tyall])
                    y2 = yall.rearrange("p a b -> p (a b)")
                    tt("dve", y2, y2, gnw, ALU.mult, [Tyall, Tct2], [Tyall])
                    tt("dve", ymix, y2, rgc[i], ALU.mult, [Tyall, Tg2[i]], [Tymix])
                    mi = m % 2
                    for g in range(2):
                        pv = psb(g)
                        for j in range(8):
                            c = g * 8 + j
                            tr(pv[:, j * 128:(j + 1) * 128], ymix[:, c * 128:(c + 1) * 128], identb,
                               [Tymix, Tidb], [Tps[g]], inc=(j == 7))
                        cp("act", mst[mi][:, g * 8:(g + 1) * 8, :], pv.rearrange("p (j t) -> p j t", j=8),
                           [Tps[g]], [Tmst[mi]])
                    load_split(mixT[0:16, :, m * 128:(m + 1) * 128].rearrange("c p t -> p c t"), mst[mi],
                               [Tmst[mi]], [TD("mixT")], Tmst[mi])
            if pend is not None:
                p2_s2(*pend)

        if stop_after >= 3:
            new_phase()
            kTh = [AR.alloc([128, TL], BF16) for _ in range(2)]
            vh = [AR.alloc([128, NL, 128], BF16) for _ in range(2)]
            qTh = [AR.alloc([128, TOWN], BF16) for _ in range(2)]
            Tk3, Tv3, Tq3 = [T("k30"), T("k31")], [T("v30"), T("v31")], [T("q30"), T("q31")]
            NWS = 4
            NW = 2 * NWS
            e_b = [AR.alloc([128, 512], F32) for _ in range(NW)]
            sp_b = [AR.alloc([128, 512], F32) for _ in range(NW)]
            spm_b = [AR.alloc([128, 512], BF16) for _ in range(NW)]
            Sb_b = [AR.alloc([128, 512], BF16) for _ in range(NW)]
            w_b = [AR.alloc([128, 512], BF16) for _ in range(NW)]
            Te, Tsp, Tspm, TSb, Tw = ([T(f"e{i}") for i in range(NW)], [T(f"sp{i}") for i in range(NW)],
                                      [T(f"spm{i}") for i in range(NW)], [T(f"Sb{i}") for i in range(NW)],
                                      [T(f"w{i}") for i in range(NW)])
            negtri = AR.alloc([128, 128], BF16)
            negones = AR.alloc([128, 128], BF16)
            cmaskb = AR.alloc([128, 128], BF16)
            ones32 = AR.alloc([128, 128], F32)
            tmpc = AR.alloc([128, 2, 128], F32)
            Tct3 = T("ct3")
            Ttmpc = T("tmpc")
            load(tmpc[:, 0, :], negtri_d, [Tin], [Ttmpc], Ttmpc)
            load(tmpc[:, 1, :], cmask_d, [Tin], [Ttmpc], Ttmpc)
            cp("dve", negtri, tmpc[:, 0, :], [Ttmpc], [Tct3])
            cp("dve", cmaskb, tmpc[:, 1, :], [Ttmpc], [Tct3])
            mset("dve", negones, -1.0, [Tct3])
            mset("dve", ones32, 1.0, [Tct3])
            sq3 = [AR.alloc([128, 512], F32) for _ in range(2)]
            rr3 = [AR.alloc([128, 512], F32) for _ in range(2)]
            y3 = [AR.alloc([128, 512], BF16) for _ in range(2)]
            Tsq3, Trr3, Ty3 = [T("sq30"), T("sq31")], [T("rr30"), T("rr31")], [T("y30"), T("y31")]
            groups = [[0], [1, 2, 3, 4], [5, 6, 7, 8], [9, 10, 11, 12], [13, 14, 15, 16]]

            def p3_loads(h):
                i = h % 2
                load(kTh[i], skT[h], [TD("skT")], [Tk3[i]], Tk3[i])
                load_split(vh[i], sv[:, h * 128:(h + 1) * 128].rearrange("(c p) d -> p c d", p=128), [TD("sv")], [Tv3[i]], Tv3[i])
                load(qTh[i], sqT[h], [TD("sqT")], [Tq3[i]], Tq3[i])

            NS = 2
            L2, L3 = 2, 4

            def make_group(h, i, ms, sidx):
                bA, bB, bC, bD = sidx, 2 + sidx, 4 + sidx, 6 + sidx
                nq = len(ms) * 128
                q0 = ms[0] * 128
                ctop = NKV + ms[-1]
                info = []
                for k, c in enumerate(range(ctop, -1, -1)):
                    nbelow = sum(1 for m_ in ms if NKV + m_ < c)
                    lo = nbelow * 128
                    diag = (c >= NKV) and ((c - NKV) in ms)
                    clo = lo + (128 if diag else 0)
                    info.append(dict(c=c, lo=lo, diag=diag, clo=clo, wi=sidx * NWS + k % NWS, k=k))
                n_it = len(info)

                def st1(d):
                    c, lo, wi = d["c"], d["lo"], d["wi"]
                    mm(ps[:, bA, lo:nq], kTh[i][:, c * 128:(c + 1) * 128], qTh[i][:, q0 + lo:q0 + nq], True, True,
                       [Tk3[i], Tq3[i]], [Tps[bA]], inc=True)
                    act(e_b[wi][:, lo:nq], ps[:, bA, lo:nq], AF.Exp, [Tps[bA]], [Te[wi]])

                def st1b(d):
                    c, lo, wi = d["c"], d["lo"], d["wi"]
                    act(sp_b[wi][:, lo:nq], e_b[wi][:, lo:nq], AF.Ln, [Te[wi]], [Tsp[wi]], bias=1.0)
                    tsc("dve", spm_b[wi][:, lo:nq], sp_b[wi][:, lo:nq], kval[:, c:c + 1], ALU.mult,
                        [Tsp[wi], Tc], [Tspm[wi]])
                    if d["diag"]:
                        tt("dve", spm_b[wi][:, lo:lo + 128], spm_b[wi][:, lo:lo + 128], cmaskb, ALU.mult,
                           [Tspm[wi], Tct3], [Tspm[wi]])

                def st2(d):
                    c, lo, wi, clo, k = d["c"], d["lo"], d["wi"], d["clo"], d["k"]
                    mm(ps[:, bB, lo:nq], negtri, spm_b[wi][:, lo:nq], True, False, [Tct3, Tspm[wi]], [Tps[bB]],
                       inc=False, skip_group_check=True)
                    if clo < nq:
                        pw = info[k - 1]["wi"]
                        mm(ps[:, bB, clo:nq], identb, Sb_b[pw][:, clo:nq], False, False, [Tidb, TSb[pw]], [Tps[bB]],
                           inc=False, skip_group_check=True)
                    mm(ps[:, bB, lo:nq], kTh[i][:, c * 128:(c + 1) * 128], qTh[i][:, q0 + lo:q0 + nq], False, True,
                       [Tk3[i], Tq3[i]], [Tps[bB]], inc=True, skip_group_check=True)
                    if c > 0:
                        mm(ps[:, bD, lo:nq], negones, spm_b[wi][:, lo:nq], k == 0, True, [Tct3, Tspm[wi]], [Tps[bD]],
                           inc=True, skip_group_check=True)
                        cp("dve", Sb_b[wi][:, lo:nq], ps[:, bD, lo:nq], [Tps[bD]], [TSb[wi]])
                    act(w_b[wi][:, lo:nq], ps[:, bB, lo:nq], AF.Exp, [Tps[bB]], [Tw[wi]])
                    if d["diag"]:
                        tt("dve", w_b[wi][:, lo:lo + 128], w_b[wi][:, lo:lo + 128], cmaskb, ALU.mult,
                           [Tw[wi], Tct3], [Tw[wi]])

                def st3(d):
                    c, lo, wi, k = d["c"], d["lo"], d["wi"], d["k"]
                    mm(ps[:, bC, lo:nq], vh[i][:, c, :], w_b[wi][:, lo:nq], k == 0, c == 0, [Tv3[i], Tw[wi]], [Tps[bC]],
                       inc=True, skip_group_check=True)

                def step(t):
                    if t < n_it:
                        st1(info[t])
                    if 0 <= t - L2 < n_it:
                        st2(info[t - L2])
                    if t < n_it:
                        st1b(info[t])
                    if 0 <= t - L3 < n_it:
                        st3(info[t - L3])

                def finish():
                    g2 = sidx
                    act(sq3[g2][:, 0:nq], ps[:, bC, 0:nq], AF.Square, [Tps[bC]], [Tsq3[g2]])
                    mm(ps[:, bD, 0:nq], ones32, sq3[g2][:, 0:nq], True, True, [Tct3, Tsq3[g2]], [Tps[bD]], inc=True)
                    act(sq3[g2][:, 0:nq], ps[:, bD, 0:nq], AF.Sqrt, [Tps[bD]], [Tsq3[g2]], scale=1.0 / 128, bias=EPS)
                    recip(rr3[g2][:, 0:nq], sq3[g2][:, 0:nq], [Tsq3[g2]], [Trr3[g2]])
                    stt("dve", y3[g2][:, 0:nq], ps[:, bC, 0:nq], sbw[:, h:h + 1], rr3[g2][:, 0:nq], ALU.mult, ALU.mult,
                        [Tps[bC], Tc, Trr3[g2]], [Ty3[g2]])
                    load(mixT[16 + h, :, q0:q0 + nq], y3[g2][:, 0:nq], [Ty3[g2]], [TD("mixT")], Ty3[g2])

                return n_it + L3, step, finish

            rounds = [[groups[1], groups[2]], [groups[3], groups[4]]]
            p3_loads(0)
            for h in range(SH):
                i = h % 2
                if h + 1 < SH:
                    p3_loads(h + 1)
                for rnd in [[groups[0]]] + rounds:
                    gs = [make_group(h, i, ms, sidx) for sidx, ms in enumerate(rnd)]
                    for t in range(max(g[0] for g in gs)):
                        for g in gs:
                            if t < g[0]:
                                g[1](t)
                    for g in gs:
                        g[2]()

        if stop_after >= 4:
            new_phase()
            mxp = AR.alloc([128, DC, 512], BF16)
            Tmxp = T("mxp")
            wp = [AR.alloc([128, 8, 512], BF16) for _ in range(3)]
            Twp = [T(f"wp{i}") for i in range(3)]
            stg = [AR.alloc([128, 512], F32) for _ in range(4)]
            Tstg = [T(f"stg{i}") for i in range(4)]
            passes = [(0, 1)] + [(1 + 4 * p, 4) for p in range(4)]
            wq4 = 0
            sq4 = 0
            cbq = 0
            for (c0, nch) in passes:
                n = nch * 128
                load_split(mxp[:, :, 0:n], mixT[:, :, c0 * 128:c0 * 128 + n].rearrange("c p t -> p c t"),
                           [TD("mixT")], [Tmxp], Tmxp)
                for cb in range(8):
                    bb = (cbq % 2) * 4
                    cbq += 1
                    for kg in range(4):
                        s = wq4 % 3
                        wq4 += 1
                        wload(wp[s].rearrange("p a b -> p (a b)"), wout_d[cb][:, kg * 8 * 512:(kg + 1) * 8 * 512], Twp[s])
                        for tl in range(nch):
                            for kc in range(8):
                                k = kg * 8 + kc
                                mm(ps[:, bb + tl, :], mxp[:, k, tl * 128:(tl + 1) * 128], wp[s][:, kc, :],
                                   k == 0, k == DC - 1, [Tmxp, Twp[s]], [Tps[bb + tl]], inc=(kc == 7))
                    for tl in range(nch):
                        si = sq4 % 4
                        sq4 += 1
                        if tl % 2 == 0:
                            act(stg[si], ps[:, bb + tl, :], AF.Copy, [Tps[bb + tl]], [Tstg[si]])
                        else:
                            cp("dve", stg[si], ps[:, bb + tl, :], [Tps[bb + tl]], [Tstg[si]])
                        r0 = (c0 + tl) * 128
                        load(a_raw[r0:r0 + 128, cb * 512:(cb + 1) * 512], stg[si], [Tstg[si]], [TD("a_raw", c0 + tl)], Tstg[si])

        if stop_after >= 5:
            new_phase()
            NB5 = 3
            at = [AR.alloc([128, D], F32) for _ in range(NB5)]
            xt5 = [AR.alloc([128, D], F32) for _ in range(NB5)]
            xn5 = [AR.alloc([128, D], BF16) for _ in range(NB5)]
            Tat = [T(f"at{i}") for i in range(NB5)]
            Txt5 = [T(f"x5{i}") for i in range(NB5)]
            Txn5 = [T(f"n5{i}") for i in range(NB5)]
            wpb = AR.alloc([128, D], F32)
            Twpb = T("wpb")
            load(wpb, wpost_d.partition_broadcast(128), [Tin], [Twpb], Twpb)
            st5 = [AR.alloc([128, 8], F32) for _ in range(NB5)]
            Tst5 = [T(f"s5{i}") for i in range(NB5)]
            h2s = [AR.alloc([128, DC, 128], BF16) for _ in range(2)]
            Th2s = [T("h2s0"), T("h2s1")]

            def p5_loads(m):
                i = m % NB5
                load(at[i], a_raw[m * 128:(m + 1) * 128, :], [TD("a_raw", m)], [Tat[i]], Tat[i])
                load(xt5[i], xown[m * 128:(m + 1) * 128, :], [Tin], [Txt5[i]], Txt5[i])

            p5_loads(0)
            p5_loads(1)
            for m in range(NOWN):
                i = m % NB5
                hi = m % 2
                r0 = m * 128
                if m + 2 < NOWN:
                    p5_loads(m + 2)
                mset("dve", st5[i][:, 0:1], 0.0, [Tst5[i]])
                act(xn5[i], at[i], AF.Square, [Tat[i], Tst5[i]], [Txn5[i], Tst5[i]], accum_out=st5[i][:, 0:1])
                rstd_from_ssq(st5[i][:, 0:1], st5[i][:, 1:2], st5[i][:, 2:3], D, Tst5[i])
                stt("dve", at[i], at[i], st5[i][:, 2:3], wpb, ALU.mult, ALU.mult, [Tat[i], Tst5[i], Twpb], [Tat[i]])
                tt("dve", xt5[i], xt5[i], at[i], ALU.add, [Txt5[i], Tat[i]], [Txt5[i]])
                load(h1_d[r0:r0 + 128, :], xt5[i], [Txt5[i]], [TD("h1", m)], Txt5[i], q="pool")
                norm_transpose(None, xt5[i], xn5[i], st5[i][:, 4:8], Txt5[i], Txn5[i], Tst5[i], wffnpre,
                               lambda c0, hi=hi: h2s[hi][:, c0:c0 + 8, :], [], [Th2s[hi]], (0, 1))
                load_split(h2T[:, :, r0:r0 + 128].rearrange("c p t -> p c t"), h2s[hi], [Th2s[hi]], [TD("h2T")], Th2s[hi],
                           q="pool")

        if stop_after >= 6:
            new_phase()
            h2p = AR.alloc([128, DC, 512], BF16)
            Th2p = T("h2p")
            actT = AR.alloc([128, FB, 512], BF16)
            TactT = T("actT")
            wreg = AR.alloc([128, 4, 4096], BF16)
            Twr = [T(f"wr{i}") for i in range(4)]
            graw = [AR.alloc([128, 514], F32) for _ in range(2)]
            tcv = [AR.alloc([128, 512], F32) for _ in range(2)]
            scv = [AR.alloc([128, 512], F32) for _ in range(2)]
            Tgraw, Ttcv, Tscv = [T("gr0"), T("gr1")], [T("tc0"), T("tc1")], [T("sc0"), T("sc1")]
            halo = AR.alloc([128, 2, FB, 2], F32)
            Thalo = [T("halo0"), T("halo1")]
            cw = AR.alloc([128, FB, 3], F32)
            cbv = AR.alloc([128, FB], F32)
            Tcv = T("convw")
            load(cw.rearrange("p a b -> p (a b)"), cw_d, [Tin], [Tcv], Tcv)
            load(cbv, cb_d, [Tin], [Tcv], Tcv)
            stg6 = [AR.alloc([128, 512], F32) for _ in range(4)]
            Tstg6 = [T(f"sg{i}") for i in range(4)]
            h2h = AR.alloc([128, DC, 128], BF16)
            Th2h = T("h2h")
            load_split(h2h, h2T[:, :, 0:128].rearrange("c p t -> p c t"), [TD("h2T")], [Th2h], Th2h)
            wq6 = 0
            pq = 0
            sq6 = 0
            for p in range(4):
                hin, hout = p % 2, (p + 1) % 2
                c0 = 1 + 4 * p
                load_split(h2p, h2T[:, :, c0 * 128:c0 * 128 + 512].rearrange("c p t -> p c t"), [TD("h2T")], [Th2p], Th2p)
                for fb in range(FB):
                    s2 = (pq % 2) * 2
                    u = pq % 2
                    pq += 1
                    wload(wreg[:, s2, :], wup_d[fb], Twr[s2])
                    wload(wreg[:, s2 + 1, :], wup_d[FB + fb], Twr[s2 + 1])
                    bG, bU = 2 * u, 2 * u + 1
                    for (sl, bk) in ((s2, bG), (s2 + 1, bU)):
                        w3 = wreg[:, sl, :].rearrange("p (k c) -> p k c", k=DC)
                        for kc in range(DC):
                            mm(ps[:, bk, :], w3[:, kc, :], h2p[:, kc, :], kc == 0, kc == DC - 1,
                               [Twr[sl], Th2p], [Tps[bk]], inc=(kc == DC - 1))
                    g = graw[u]
                    if p == 0:
                        hb = 4 + u
                        w3g = wreg[:, s2, :].rearrange("p (k c) -> p k c", k=DC)
                        for kc in range(DC):
                            mm(ps[:, hb, 0:2], w3g[:, kc, :], h2h[:, kc, 126:128], kc == 0, kc == DC - 1,
                               [Twr[s2], Th2h], [Tps[hb]], inc=(kc == DC - 1))
                    act(g[:, 2:514], ps[:, bG, :], AF.Copy, [Tps[bG]], [Tgraw[u]])
                    if p == 0:
                        act(g[:, 0:2], ps[:, hb, 0:2], AF.Copy, [Tps[hb]], [Tgraw[u]])
                    else:
                        cp("dve", g[:, 0:2], halo[:, hin, fb, :], [Thalo[hin]], [Tgraw[u]])
                    cp("dve", halo[:, hout, fb, :], g[:, 512:514], [Tgraw[u]], [Thalo[hout]])
                    tsc("dve", tcv[u], g[:, 2:514], cw[:, fb, 2:3], ALU.mult, [Tgraw[u], Tcv], [Ttcv[u]],
                        s2=cbv[:, fb:fb + 1], op1=ALU.add)
                    stt("dve", tcv[u], g[:, 1:513], cw[:, fb, 1:2], tcv[u], ALU.mult, ALU.add, [Tgraw[u], Tcv, Ttcv[u]], [Ttcv[u]])
                    stt("dve", tcv[u], g[:, 0:512], cw[:, fb, 0:1], tcv[u], ALU.mult, ALU.add, [Tgraw[u], Tcv, Ttcv[u]], [Ttcv[u]])
                    act(scv[u], tcv[u], AF.Silu, [Ttcv[u]], [Tscv[u]])
                    tt("dve", actT[:, fb, :], scv[u], ps[:, bU, :], ALU.mult, [Tscv[u], Tps[bU]], [TactT])
                pieces = [(k0, min(8, FB - k0)) for k0 in range(0, FB, 8)]
                for cb in range(8):
                    for (k0, nk) in pieces:
                        s = wq6 % 4
                        wq6 += 1
                        wload(wreg[:, s, 0:nk * 512], wdn_d[cb][:, k0 * 512:(k0 + nk) * 512], Twr[s])
                        w3 = wreg[:, s, :].rearrange("p (k c) -> p k c", k=8)
                        for tl in range(4):
                            for kc in range(nk):
                                k = k0 + kc
                                mm(ps[:, 4 + tl, :], actT[:, k, tl * 128:(tl + 1) * 128], w3[:, kc, :],
                                   k == 0, k == FB - 1, [TactT, Twr[s]], [Tps[4 + tl]], inc=(kc == nk - 1))
                    for tl in range(4):
                        si = sq6 % 4
                        sq6 += 1
                        if tl % 2 == 0:
                            act(stg6[si], ps[:, 4 + tl, :], AF.Copy, [Tps[4 + tl]], [Tstg6[si]])
                        else:
                            cp("dve", stg6[si], ps[:, 4 + tl, :], [Tps[4 + tl]], [Tstg6[si]])
                        ch = 4 * p + tl
                        load(f_raw[ch * 128:(ch + 1) * 128, cb * 512:(cb + 1) * 512], stg6[si], [Tstg6[si]],
                             [TD("f_raw", ch)], Tstg6[si])

        outs = []
        if stop_after >= 7:
            new_phase()
            NB7 = 3
            ft = [AR.alloc([128, D], F32) for _ in range(NB7)]
            ht = [AR.alloc([128, D], F32) for _ in range(NB7)]
            jk = AR.alloc([128, D], BF16)
            Tft = [T(f"ft{i}") for i in range(NB7)]
            Tht = [T(f"ht{i}") for i in range(NB7)]
            Tjk = T("jk")
            wfb = AR.alloc([128, D], F32)
            Twfb = T("wfb")
            load(wfb, wffnpost_d.partition_broadcast(128), [Tin], [Twfb], Twfb)
            st7 = [AR.alloc([128, 4], F32) for _ in range(NB7)]
            Tst7 = [T(f"s7{i}") for i in range(NB7)]

            def p7_loads(ch):
                i = ch % NB7
                load(ft[i], f_raw[ch * 128:(ch + 1) * 128, :], [TD("f_raw", ch)], [Tft[i]], Tft[i])
                load(ht[i], h1_d[(ch + 1) * 128:(ch + 2) * 128, :], [TD("h1", ch + 1)], [Tht[i]], Tht[i])

            p7_loads(0)
            p7_loads(1)
            for ch in range(16):
                i = ch % NB7
                if ch + 2 < 16:
                    p7_loads(ch + 2)
                mset("dve", st7[i][:, 0:1], 0.0, [Tst7[i]])
                act(jk, ft[i], AF.Square, [Tft[i], Tst7[i]], [Tjk, Tst7[i]], accum_out=st7[i][:, 0:1])
                rstd_from_ssq(st7[i][:, 0:1], st7[i][:, 1:2], st7[i][:, 2:3], D, Tst7[i])
                stt("dve", ft[i], ft[i], st7[i][:, 2:3], wfb, ALU.mult, ALU.mult, [Tft[i], Tst7[i], Twfb], [Tft[i]])
                tt("dve", ht[i], ht[i], ft[i], ALU.add, [Tht[i], Tft[i]], [Tht[i]])
                outs.append(load(out_d[ch * 128:(ch + 1) * 128, :], ht[i], [Tht[i]], [TD("out", ch)], Tht[i], q="pool"))

        S.barrier()
        print(f"[kernel] instructions={S.n_ins} waits={S.n_wait} sems={S.nsem}")
        S.run_block()
    return nc


def _consts():
    ident = np.eye(128, dtype=np.float32)
    k = np.arange(128)
    negtri = np.where(k[:, None] >= k[None, :], -1.0, 0.0).astype(np.float32)
    cmask = np.where(k[:, None] < k[None, :], 1.0, 0.0).astype(np.float32)
    gam = (1.0 - 2.0 ** (-5.0 - np.arange(RH, dtype=np.float64)))
    n = np.arange(128, dtype=np.float64)
    gq = (256.0 ** -0.5) * gam[:, None] ** (n[None, :] + 1.0)
    gq = np.broadcast_to(gq.reshape(1, RH * 128), (128, RH * 128)).astype(np.float32)
    dm = np.zeros((128, RH, 128), np.float64)
    for h in range(RH):
        dm[:, h, :] = np.where(n[None, :] >= n[:, None], gam[h] ** (-(n[:, None] + 1.0)), 0.0)
    dm = dm.reshape(128, RH * 128).astype(np.float32)
    kdec = (gam[None, :] ** (127.0 - n[:, None])).astype(np.float32)
    return ident, negtri, cmask, gq, dm, kdec


def _rope_tables(pos):
    half = 128
    inv = (10000.0 ** (-np.arange(half, dtype=np.float32) / half)).astype(np.float32)
    ang = pos.astype(np.float32)[None, :] * inv[:, None]
    return np.stack([np.cos(ang), np.sin(ang)]).astype(np.float32)


def _prep_shared(inputs):
    f = lambda a: np.ascontiguousarray(np.asarray(a, dtype=np.float32))
    w_in = f(inputs["w_in"])[0]
    w_out = f(inputs["w_out"])[0]
    w_up = f(inputs["w_up"])[0]
    w_down = f(inputs["w_down"])[0]
    sh = {}
    sh["win"] = np.ascontiguousarray(w_in.reshape(DC, 128, 112, 128).transpose(2, 1, 0, 3)).reshape(112, 128, DC * 128)
    sh["wout"] = np.ascontiguousarray(w_out.reshape(DC, 128, 8, 512).transpose(2, 1, 0, 3)).reshape(8, 128, DC * 512)
    sh["wup"] = np.ascontiguousarray(w_up.reshape(DC, 128, 2 * FB, 128).transpose(2, 1, 0, 3)).reshape(2 * FB, 128, DC * 128)
    sh["wdn"] = np.ascontiguousarray(w_down.reshape(FB, 128, 8, 512).transpose(2, 1, 0, 3)).reshape(8, 128, FB * 512)
    colmaj = lambda v, n: np.ascontiguousarray(f(v).reshape(n, 128).T)
    sh["wpre"] = colmaj(inputs["attn_pre_norm_w"][0], DC)
    sh["wffnpre"] = colmaj(inputs["ffn_pre_norm_w"][0], DC)
    sh["wpost"] = f(inputs["attn_post_norm_w"]).reshape(1, D)
    sh["wffnpost"] = f(inputs["ffn_post_norm_w"]).reshape(1, D)
    sh["gnw"] = f(inputs["ret_gn_w"]).reshape(1, 2048)
    sh["sbw"] = colmaj(inputs["sb_norm_w"][0], SH)
    cwv = f(inputs["conv_w"])[0]
    sh["cw"] = np.ascontiguousarray(cwv.reshape(3, FB, 128).transpose(2, 1, 0)).reshape(128, FB * 3)
    sh["cb"] = colmaj(inputs["conv_b"][0], FB)
    ident, negtri, cmask, gq, dm, kdec = _consts()
    sh.update(ident=ident, negtri=negtri, cmask=cmask, gq=gq, dm=dm, kdec=kdec)
    return sh


def _prep_core(inputs, b, j):
    x = np.asarray(inputs["x"], dtype=np.float32)
    meta = np.asarray(inputs["meta_tokens"], dtype=np.float32)
    metachunk = np.zeros((128, D), np.float32)
    metachunk[112:] = meta
    kval = np.ones((128, NL), np.float32)
    if j == 0:
        xown = np.concatenate([metachunk, x[b, 0:2048]], axis=0)
        xkv = np.zeros((TKV, D), np.float32)
        kval[:, 0:NKV] = 0.0
        kval[0:112, NKV] = 0.0
        own_pc0 = 0
        kv_pos = np.zeros(TKV, np.float32)
    else:
        xown = np.ascontiguousarray(x[b, 15 * 128:4096])
        xkv = np.concatenate([metachunk, x[b, 0:15 * 128]], axis=0)
        kval[0:112, 0] = 0.0
        own_pc0 = 16
        kv_pos = np.arange(TKV, dtype=np.float32) - 112.0
    own_pos = np.arange(TOWN, dtype=np.float32) + own_pc0 * 128 - 112.0
    return {
        "xown": np.ascontiguousarray(xown), "xkv": np.ascontiguousarray(xkv),
        "cs_own": _rope_tables(own_pos), "cs_kv": _rope_tables(kv_pos), "kval": kval,
    }


_CACHE = {}


def kernel(**inputs):
    stop_after = int(os.environ.get("MK_STOP", "99"))
    debug = bool(int(os.environ.get("MK_DEBUG", "0")))
    key = (stop_after, debug)
    if key not in _CACHE:
        _CACHE[key] = build_program(stop_after, debug)
    nc = _CACHE[key]
    sh = _prep_shared(inputs)
    ncores = int(os.environ.get("MK_NCORES", "8"))
    in_maps = []
    for c in range(ncores):
        b, j = divmod(c, 2)
        m = dict(sh)
        m.update(_prep_core(inputs, b, j))
        in_maps.append(m)
    res = run_bass_kernel_spmd(nc, in_maps, core_ids=list(range(ncores)))
    if debug:
        kernel.last_results = res.results
    out = np.zeros((BATCH, SEQ, D), np.float32)
    for c in range(ncores):
        b, j = divmod(c, 2)
        out[b, j * 2048:(j + 1) * 2048] = np.asarray(res.results[c]["out"], dtype=np.float32)
    return out
```

```python
import contextlib
import math
import os
import numpy as np
import concourse.bass as bass
import concourse.mybir as mybir
from concourse.bass_utils import run_bass_kernel_spmd

F32 = mybir.dt.float32
BF16 = mybir.dt.bfloat16
AF = mybir.ActivationFunctionType
ALU = mybir.AluOpType

SEM_LIMIT = 30000
STRICT_SAME_ENGINE = True

D = 4096
DC = 32
BATCH = 4
SEQ = 4096
N_META = 16
RH, SH = 8, 16
IN_COLS = 14336
D_FF = 11008
FB = 86
EPS = 1e-6
NOWN, NKV, NL = 17, 16, 33
TOWN, TKV, TL = NOWN * 128, NKV * 128, NL * 128
NOUT = 16 * 128


class Tok:
    __slots__ = ("eng", "sem", "val", "owner")

    def __init__(self, eng, sem=None, val=None, owner=None):
        self.eng = eng
        self.sem = sem
        self.val = val
        self.owner = owner


class T:
    __slots__ = ("name", "w", "r", "dsem", "dcnt")

    def __init__(self, name=""):
        self.name = name
        self.w = None
        self.r = {}
        self.dsem = None
        self.dcnt = 0


class Sched:
    ENGS = ("pe", "act", "dve", "pool", "sp")

    def __init__(self, nc, stack):
        self.nc = nc
        self.stack = stack
        self.q = {e: [] for e in self.ENGS}
        self.cnt = {e: 0 for e in self.ENGS}
        self.sem = {e: None for e in self.ENGS}
        self.known = {e: {} for e in self.ENGS}
        self.pending_pe = []
        self.owners = []
        self.nsem = 0
        self.n_wait = 0
        self.n_ins = 0

    def new_sem(self, name):
        self.nsem += 1
        return self.stack.enter_context(self.nc.semaphore(f"{name}_{self.nsem}"))

    def _eng_sem(self, e):
        if self.sem[e] is None or self.cnt[e] >= SEM_LIMIT:
            self.sem[e] = self.new_sem("e" + e)
            self.cnt[e] = 0
        return self.sem[e]

    def _waits(self, e, reads, writes):
        need = {}

        def add(tok, kind):
            if tok is None:
                return
            if tok.eng == e:
                if e == "pe" or e == "sp":
                    return
                if not STRICT_SAME_ENGINE or kind == "war":
                    return
            if tok.eng == "dma":
                val = tok.owner.dcnt * 16
            else:
                val = tok.val
                if val is None:
                    raise RuntimeError("dependency on unresolved PE token")
            k = id(tok.sem)
            if k not in need or need[k][1] < val:
                need[k] = (tok.sem, val)

        for t in reads:
            add(t.w, "raw")
        for t in writes:
            add(t.w, "waw")
            for tok in t.r.values():
                add(tok, "war")
        return self._filter(e, need)

    def _filter(self, e, need):
        out = []
        kn = self.known[e]
        for k, (sem, val) in need.items():
            if kn.get(k, 0) >= val:
                continue
            kn[k] = val
            out.append((sem, val))
        return out

    def _record(self, tok, key, reads, writes):
        for t in writes:
            t.w = tok
            t.r = {}
        for t in reads:
            t.r[key] = tok

    def op(self, e, fn, reads=(), writes=(), inc=True):
        for (sem, val) in self._waits(e, reads, writes):
            self.q[e].append(("wait", sem, val))
            self.n_wait += 1
        self.n_ins += 1
        if inc:
            sem = self._eng_sem(e)
            self.cnt[e] += 1
            tok = Tok(e, sem, self.cnt[e])
            self.q[e].append(("ins", fn, sem, 1))
            if e == "pe":
                for p in self.pending_pe:
                    p.sem, p.val = sem, self.cnt[e]
                self.pending_pe = []
        else:
            assert e == "pe"
            tok = Tok(e, None, None)
            self.pending_pe.append(tok)
            self.q[e].append(("ins", fn, None, 0))
        self._record(tok, e, reads, writes)
        return tok

    def dma(self, e, out, in_, reads, writes, owner, **kw):
        for (sem, val) in self._waits(e, reads, writes):
            self.q[e].append(("wait", sem, val))
            self.n_wait += 1
        if owner.dsem is None:
            owner.dsem = self.new_sem("d")
            self.owners.append(owner)
        owner.dcnt += 1
        tok = Tok("dma", owner.dsem, owner.dcnt * 16, owner)
        self.q[e].append(("ins", lambda h, o=out, i=in_, k=kw: h.dma_start(out=o, in_=i, **k),
                          owner.dsem, 16))
        self.n_ins += 1
        self._record(tok, ("dma", id(owner.dsem)), reads, writes)
        return tok

    def barrier(self):
        assert not self.pending_pe
        allw = {}
        for e in self.ENGS:
            if self.sem[e] is not None and self.cnt[e] > 0:
                allw[id(self.sem[e])] = (self.sem[e], self.cnt[e], e)
        for o in self.owners:
            if o.dcnt:
                allw[id(o.dsem)] = (o.dsem, o.dcnt * 16, "dma")
        for e in self.ENGS:
            need = {k: (s, v) for k, (s, v, src) in allw.items() if src != e}
            for (sem, val) in self._filter(e, need):
                self.q[e].append(("wait", sem, val))
                self.n_wait += 1

    def replay(self, e, h):
        for item in self.q[e]:
            if item[0] == "wait":
                h.wait_ge(item[1], item[2])
            else:
                ins = item[1](h)
                if item[2] is not None:
                    ins.then_inc(item[2], item[3])

    def run_block(self):
        with self.nc.Block() as block:
            @block.tensor
            def _(h):
                self.replay("pe", h)

            @block.scalar
            def _(h):
                self.replay("act", h)

            @block.vector
            def _(h):
                self.replay("dve", h)

            @block.gpsimd
            def _(h):
                self.replay("pool", h)

            @block.sync
            def _(h):
                self.replay("sp", h)


class Arena:
    def __init__(self, big, nbytes):
        self.big = big
        self.nbytes = nbytes
        self.off = 0

    def reset(self):
        self.off = 0

    def alloc(self, shape, dt):
        size = 4 if dt == F32 else 2
        n = 1
        for s in shape[1:]:
            n *= s
        nb = (n * size + 63) // 64 * 64
        assert self.off + nb <= self.nbytes, f"SBUF arena overflow {self.off + nb} > {self.nbytes}"
        v = self.big[:, self.off // 2: self.off // 2 + n * size // 2]
        self.off += nb
        if dt == F32:
            v = v.bitcast(F32)
        if len(shape) == 3:
            v = v.rearrange("p (a b) -> p a b", a=shape[1])
        elif len(shape) == 4:
            v = v.rearrange("p (a b c) -> p a b c", a=shape[1], b=shape[2])
        return v


def build_program(stop_after=99, debug=False):
    nc = bass.Bass("TRN2", target_bir_lowering=False)
    okind = "ExternalOutput" if debug else "Internal"

    def din(name, shape, dt=F32):
        return nc.dram_tensor(name, list(shape), dt, kind="ExternalInput").ap()

    def dscr(name, shape, dt):
        return nc.dram_tensor(name, list(shape), dt, kind=okind).ap()

    xown = din("xown", [TOWN, D])
    xkv = din("xkv", [TKV, D])
    cs_own = din("cs_own", [2, 128, TOWN])
    cs_kv = din("cs_kv", [2, 128, TKV])
    kval_d = din("kval", [128, NL])
    ident_d = din("ident", [128, 128])
    negtri_d = din("negtri", [128, 128])
    cmask_d = din("cmask", [128, 128])
    gq_d = din("gq", [128, RH * 128])
    dm_d = din("dm", [128, RH * 128])
    kdec_d = din("kdec", [128, RH])
    wpre_d = din("wpre", [128, DC])
    wffnpre_d = din("wffnpre", [128, DC])
    wpost_d = din("wpost", [1, D])
    wffnpost_d = din("wffnpost", [1, D])
    gnw_d = din("gnw", [1, 2048])
    sbw_d = din("sbw", [128, SH])
    cw_d = din("cw", [128, FB * 3])
    cb_d = din("cb", [128, FB])
    win_d = din("win", [112, 128, DC * 128])
    wout_d = din("wout", [8, 128, DC * 512])
    wup_d = din("wup", [2 * FB, 128, DC * 128])
    wdn_d = din("wdn", [8, 128, FB * 512])
    out_d = nc.dram_tensor("out", [NOUT, D], F32, kind="ExternalOutput").ap()

    rqT = dscr("rqT", [RH, 2, 128, TOWN], BF16)
    rkT = dscr("rkT", [RH, 2, 128, TL], BF16)
    rv = dscr("rv", [TL, 2048], BF16)
    rgs = dscr("rgs", [TOWN, 2048], BF16)
    sqT = dscr("sqT", [SH, 128, TOWN], BF16)
    skT = dscr("skT", [SH, 128, TL], BF16)
    sv = dscr("sv", [TL, 2048], BF16)
    mixT = dscr("mixT", [DC, 128, TOWN], BF16)
    a_raw = dscr("a_raw", [TOWN, D], F32)
    h1_d = dscr("h1", [TOWN, D], F32)
    h2T = dscr("h2T", [DC, 128, TOWN], BF16)
    f_raw = dscr("f_raw", [NOUT, D], F32)

    gam = [1.0 - 2.0 ** (-5.0 - h) for h in range(RH)]

    with contextlib.ExitStack() as st:
        S = Sched(nc, st)
        ARENA_BYTES = 198 * 1024
        big = st.enter_context(nc.sbuf_tensor("big", [128, ARENA_BYTES // 2], BF16))
        ps = st.enter_context(nc.psum_tensor("ps", [128, 8, 512], F32))
        AR = Arena(big, ARENA_BYTES)
        Tps = [T(f"ps{i}") for i in range(8)]
        Tin = T("in")

        def psb(b):
            return ps[:, b, :].bitcast(BF16)

        Tg = {}

        def TD(name, i=0):
            k = (name, i)
            if k not in Tg:
                Tg[k] = T(name)
            return Tg[k]

        def new_phase():
            S.barrier()
            AR.reset()

        def mm(out, lhsT, rhs, start, stop, reads, writes, inc, **kw):
            S.op("pe", lambda h, o=out, l=lhsT, r=rhs, s=start, e=stop, k=kw:
                 h.matmul(o, lhsT=l, rhs=r, start=s, stop=e, **k), reads, writes, inc=inc)

        def tr(out, in_, ident, reads, writes, inc):
            S.op("pe", lambda h, o=out, i=in_, d=ident: h.transpose(out=o, in_=i, identity=d),
                 reads, writes, inc=inc)

        def act(out, in_, func, reads, writes, **kw):
            S.op("act", lambda h, o=out, i=in_, f=func, k=kw: h.activation(out=o, in_=i, func=f, **k),
                 reads, writes)

        def tt(eng, out, in0, in1, op, reads, writes):
            S.op(eng, lambda h, o=out, a=in0, b=in1, p=op: h.tensor_tensor(out=o, in0=a, in1=b, op=p),
                 reads, writes)

        def tsc(eng, out, in0, s1, op0, reads, writes, s2=None, op1=None):
            if op1 is None:
                S.op(eng, lambda h, o=out, a=in0, s=s1, p=op0: h.tensor_scalar(out=o, in0=a, scalar1=s, scalar2=None, op0=p),
                     reads, writes)
            else:
                S.op(eng, lambda h, o=out, a=in0, s=s1, z=s2, p=op0, q=op1:
                     h.tensor_scalar(out=o, in0=a, scalar1=s, scalar2=z, op0=p, op1=q), reads, writes)

        def stt(eng, out, in0, scalar, in1, op0, op1, reads, writes):
            S.op(eng, lambda h, o=out, a=in0, s=scalar, b=in1, p=op0, q=op1:
                 h.scalar_tensor_tensor(out=o, in0=a, scalar=s, in1=b, op0=p, op1=q), reads, writes)

        def cp(eng, out, in_, reads, writes):
            if eng == "act":
                S.op(eng, lambda h, o=out, i=in_: h.activation(out=o, in_=i, func=AF.Copy), reads, writes)
            else:
                S.op(eng, lambda h, o=out, i=in_: h.tensor_copy(out=o, in_=i), reads, writes)

        def mset(eng, ap, val, writes):
            S.op(eng, lambda h, a=ap, v=val: h.memset(a, v), [], writes)

        def recip(out, in_, reads, writes):
            S.op("dve", lambda h, o=out, i=in_: h.reciprocal(out=o, in_=i), reads, writes)

        def load(out, in_, reads, writes, owner, q="sp", **kw):
            return S.dma(q, out, in_, reads, writes, owner, **kw)

        def load_split(out, in_, reads, writes, owner, rows=4, q="sp"):
            R = out.shape[1]
            tok = None
            for r0 in range(0, R, rows):
                r1 = min(R, r0 + rows)
                tok = S.dma(q, out[:, r0:r1], in_[:, r0:r1], reads, writes, owner)
            return tok

        def wload(out, in_, tslot):
            return S.dma("pool", out, in_, [Tin], [tslot], tslot, max_dma_last_dim=8192)

        def rstd_from_ssq(ssq, tmp, out, n, Tst):
            act(tmp, ssq, AF.Sqrt, [Tst], [Tst], scale=1.0 / n, bias=EPS)
            recip(out, tmp, [Tst], [Tst])

        CONST_BYTES = 8 * 1024
        cbig = st.enter_context(nc.sbuf_tensor("cbig", [128, CONST_BYTES // 2], BF16))
        CA = Arena(cbig, CONST_BYTES)
        identf = CA.alloc([128, 128], F32)
        identb = CA.alloc([128, 128], BF16)
        kval = CA.alloc([128, NL], F32)
        wpre = CA.alloc([128, DC], F32)
        wffnpre = CA.alloc([128, DC], F32)
        sbw = CA.alloc([128, SH], F32)
        kdec = CA.alloc([128, RH], F32)
        Tc = T("consts")
        load(identf, ident_d, [Tin], [Tc], Tc)
        load(kval, kval_d, [Tin], [Tc], Tc)
        load(wpre, wpre_d, [Tin], [Tc], Tc)
        load(wffnpre, wffnpre_d, [Tin], [Tc], Tc)
        load(sbw, sbw_d, [Tin], [Tc], Tc)
        load(kdec, kdec_d, [Tin], [Tc], Tc)
        Tidb = T("identb")
        cp("dve", identb, identf, [Tc], [Tidb])

        def norm_transpose(src_rows, xt, xn, st4, Txt, Txn, Tst, wvec, dst, dst_reads, dst_writes,
                           tbanks, h_in=None, h1_out=None):
            norm_part(xt, xn, st4, Txt, Txn, Tst)
            transpose_part(xn, Txn, wvec, dst, dst_reads, dst_writes, tbanks)

        def norm_part(xt, xn, st4, Txt, Txn, Tst):
            mset("dve", st4[:, 0:1], 0.0, [Tst])
            act(xn, xt, AF.Square, [Txt, Tst], [Txn, Tst], accum_out=st4[:, 0:1])
            rstd_from_ssq(st4[:, 0:1], st4[:, 1:2], st4[:, 2:3], D, Tst)
            tsc("dve", xn, xt, st4[:, 2:3], ALU.mult, [Txt, Tst], [Txn])

        def transpose_part(xn, Txn, wvec, dst, dst_reads, dst_writes, tbanks):
            for g in range(4):
                b = tbanks[g % 2]
                pv = psb(b)
                for j in range(8):
                    c = g * 8 + j
                    tr(pv[:, j * 128:(j + 1) * 128], xn[:, c * 128:(c + 1) * 128], identb,
                       [Txn, Tidb], [Tps[b]], inc=(j == 7))
                tt("dve", dst(g * 8), pv.rearrange("p (j t) -> p j t", j=8),
                   wvec[:, g * 8:(g + 1) * 8].unsqueeze(2).to_broadcast([128, 8, 128]), ALU.mult,
                   [Tps[b], Tc] + dst_reads, dst_writes)

        hT = AR.alloc([128, DC, 1152], BF16)
        wsl = [AR.alloc([128, DC * 128], BF16) for _ in range(4)]
        Tws = [T(f"ws{i}") for i in range(4)]
        xt = [AR.alloc([128, D], F32) for _ in range(2)]
        xn = [AR.alloc([128, D], BF16) for _ in range(2)]
        Txt = [T("xt0"), T("xt1")]
        Txn = [T("xn0"), T("xn1")]
        cs = AR.alloc([128, 2, 1152], F32)
        Tcs = T("cs")
        st4 = [AR.alloc([128, 4], F32) for _ in range(2)]
        Tst = [T("st0"), T("st1")]
        gq = AR.alloc([128, RH, 128], F32)
        Tgq = T("gq")
        load(gq.rearrange("p a b -> p (a b)"), gq_d, [Tin], [Tgq], Tgq)
        NR = 3
        rt = [[AR.alloc([128, 512], F32) for _ in range(3)] for _ in range(2)]
        Trt = [T("rt0"), T("rt1")]
        ob = [AR.alloc([128, 2, 512], BF16) for _ in range(NR)]
        Tob = [T(f"ob{i}") for i in range(NR)]
        tb = [AR.alloc([128, 512], BF16) for _ in range(2)]
        Ttb = [T("tb0"), T("tb1")]
        ob2 = [AR.alloc([128, 4, 128], BF16) for _ in range(2)]
        Tob2 = [T("o20"), T("o21")]
        ThT = [T(f"hT{i}") for i in range(9)]

        wq = {"n": 0}
        obq = {"n": 0, "t": 0, "r": 0}
        bankq = {"n": 0}

        def in_proj_pass(src, src_c0, nch, own):
            ntok = nch * 128
            loc0 = (NKV + src_c0 if own else src_c0)
            csrc = cs_own if own else cs_kv
            load(cs[:, :, 0:ntok], csrc[:, :, src_c0 * 128: src_c0 * 128 + ntok].rearrange("a p t -> p a t"),
                 [Tin], [Tcs], Tcs)
            for ci in range(nch):
                i = ci % 2
                r0 = (src_c0 + ci) * 128
                load(xt[i], src[r0:r0 + 128, :], [Tin], [Txt[i]], Txt[i])
                norm_transpose(None, xt[i], xn[i], st4[i], Txt[i], Txn[i], Tst[i], wpre,
                               lambda c0, ci=ci: hT[:, c0:c0 + 8, ci * 128:(ci + 1) * 128],
                               [], [ThT[ci]], (6, 7))
            tiles = [(t0, min(512, ntok - t0)) for t0 in range(0, ntok, 512)]
            allhT = [ThT[ci] for ci in range(nch)]

            def wfetch(blk):
                s = wq["n"] % 4
                wq["n"] += 1
                wload(wsl[s], win_d[blk], Tws[s])
                return s

            deferred = []

            def run_deferred():
                while deferred:
                    deferred.pop(0)()

            def gemm(s, t0, n, bank):
                w3 = wsl[s].rearrange("p (k c) -> p k c", k=DC)
                pend_ = list(deferred)
                del deferred[:]
                for kc in range(DC):
                    mm(ps[:, bank, 0:n], w3[:, kc, :], hT[:, kc, t0:t0 + n], kc == 0, kc == DC - 1,
                       [Tws[s]] + allhT, [Tps[bank]], inc=(kc == DC - 1))
                for f_ in pend_:
                    f_()

            def unit_rope(kind, h):
                base = 0 if kind == "q" else 16
                s1 = wfetch(base + 2 * h)
                s2 = wfetch(base + 2 * h + 1)
                for (t0, n) in tiles:
                    b0 = (bankq["n"] % 2) * 2
                    bankq["n"] += 1
                    gemm(s1, t0, n, b0)
                    gemm(s2, t0, n, b0 + 1)
                    ri = obq["r"] % 2
                    obq["r"] += 1
                    oi = obq["n"] % NR
                    obq["n"] += 1
                    r = rt[ri]
                    cosv, sinv = cs[:, 0, t0:t0 + n], cs[:, 1, t0:t0 + n]
                    p1, p2 = ps[:, b0, 0:n], ps[:, b0 + 1, 0:n]
                    nchk = n // 128
                    gb = gq[:, h, :].unsqueeze(1).to_broadcast([128, nchk, 128])
                    tt("dve", r[0][:, 0:n], p1, cosv, ALU.mult, [Tps[b0], Tcs], [Trt[ri]])
                    tt("dve", r[1][:, 0:n], p2, sinv, ALU.mult, [Tps[b0 + 1], Tcs], [Trt[ri]])
                    if kind == "q":
                        tt("dve", r[0][:, 0:n], r[0][:, 0:n], r[1][:, 0:n], ALU.subtract, [Trt[ri]], [Trt[ri]])
                        tt("dve", ob[oi][:, 0, 0:n].rearrange("p (a b) -> p a b", a=nchk),
                           r[0][:, 0:n].rearrange("p (a b) -> p a b", a=nchk), gb, ALU.mult,
                           [Trt[ri], Tgq], [Tob[oi]])
                    else:
                        tt("dve", ob[oi][:, 0, 0:n], r[0][:, 0:n], r[1][:, 0:n], ALU.subtract, [Trt[ri]], [Tob[oi]])
                    tt("dve", r[0][:, 0:n], p1, sinv, ALU.mult, [Tps[b0], Tcs], [Trt[ri]])
                    tt("dve", r[1][:, 0:n], p2, cosv, ALU.mult, [Tps[b0 + 1], Tcs], [Trt[ri]])
                    if kind == "q":
                        tt("dve", r[0][:, 0:n], r[0][:, 0:n], r[1][:, 0:n], ALU.add, [Trt[ri]], [Trt[ri]])
                        tt("dve", ob[oi][:, 1, 0:n].rearrange("p (a b) -> p a b", a=nchk),
                           r[0][:, 0:n].rearrange("p (a b) -> p a b", a=nchk), gb, ALU.mult,
                           [Trt[ri], Tgq], [Tob[oi]])
                        dst = rqT[h, :, :, src_c0 * 128 + t0: src_c0 * 128 + t0 + n]
                        tdst = TD("rqT")
                    else:
                        tt("dve", ob[oi][:, 1, 0:n], r[0][:, 0:n], r[1][:, 0:n], ALU.add, [Trt[ri]], [Tob[oi]])
                        dst = rkT[h, :, :, loc0 * 128 + t0: loc0 * 128 + t0 + n]
                        tdst = TD("rkT")
                    load(dst.rearrange("a p t -> p a t"), ob[oi][:, :, 0:n], [Tob[oi]], [tdst], Tob[oi])

            def unit_fm(kind, h):
                blk = (64 if kind == "sq" else 80) + h
                s = wfetch(blk)
                for (t0, n) in tiles:
                    b0 = bankq["n"] % 4
                    bankq["n"] += 1
                    gemm(s, t0, n, b0)
                    oi = obq["n"] % NR
                    obq["n"] += 1
                    if kind == "sq":
                        act(ob[oi][:, 0, 0:n], ps[:, b0, 0:n], AF.Copy, [Tps[b0]], [Tob[oi]], scale=1.0 / math.sqrt(128.0))
                        dst, tdst = sqT[h, :, src_c0 * 128 + t0: src_c0 * 128 + t0 + n], TD("sqT")
                    else:
                        act(ob[oi][:, 0, 0:n], ps[:, b0, 0:n], AF.Copy, [Tps[b0]], [Tob[oi]])
                        dst, tdst = skT[h, :, loc0 * 128 + t0: loc0 * 128 + t0 + n], TD("skT")
                    load(dst, ob[oi][:, 0, 0:n], [Tob[oi]], [tdst], Tob[oi])

            def unit_tm(kind, cb):
                blk = {"rv": 32, "rg": 48, "sv": 96}[kind] + cb
                s = wfetch(blk)
                for (t0, n) in tiles:
                    b0 = bankq["n"] % 4
                    bankq["n"] += 1
                    gemm(s, t0, n, b0)
                    ti = obq["t"] % 2
                    obq["t"] += 1
                    nchk = n // 128
                    act(tb[ti][:, 0:n], ps[:, b0, 0:n], AF.Silu if kind == "rg" else AF.Copy, [Tps[b0]], [Ttb[ti]])

                    def epi(ti=ti, n=n, nchk=nchk, t0=t0, kind=kind, cb=cb):
                        bt = 4 + ti
                        pv = psb(bt)
                        for j in range(nchk):
                            tr(pv[:, j * 128:(j + 1) * 128], tb[ti][:, j * 128:(j + 1) * 128], identb,
                               [Ttb[ti], Tidb], [Tps[bt]], inc=(j == nchk - 1))
                        pv3 = pv[:, 0:n].rearrange("p (j c) -> p j c", j=nchk)
                        lc0 = loc0 + t0 // 128
                        if kind == "rg":
                            cp("act", ob2[ti][:, 0:nchk, :], pv3, [Tps[bt]], [Tob2[ti]])
                            dst = rgs[(src_c0 * 128 + t0):(src_c0 * 128 + t0 + n), cb * 128:(cb + 1) * 128]
                            tdst = TD("rgs")
                        else:
                            tt("dve", ob2[ti][:, 0:nchk, :], pv3,
                               kval[:, lc0:lc0 + nchk].unsqueeze(2).to_broadcast([128, nchk, 128]), ALU.mult,
                               [Tps[bt], Tc], [Tob2[ti]])
                            dd = rv if kind == "rv" else sv
                            dst = dd[(loc0 * 128 + t0):(loc0 * 128 + t0 + n), cb * 128:(cb + 1) * 128]
                            tdst = TD(kind)
                        load(dst.rearrange("(j p) c -> p j c", p=128), ob2[ti][:, 0:nchk, :], [Tob2[ti]], [tdst], Tob2[ti])

                    deferred.append(epi)

            for h in range(RH):
                if own:
                    unit_rope("q", h)
                unit_rope("k", h)
            for cb in range(16):
                unit_tm("rv", cb)
            if own:
                for cb in range(16):
                    unit_tm("rg", cb)
                for h in range(SH):
                    unit_fm("sq", h)
            for h in range(SH):
                unit_fm("sk", h)
            for cb in range(16):
                unit_tm("sv", cb)
            run_deferred()

        in_proj_pass(xkv, 0, 8, False)
        in_proj_pass(xkv, 8, 8, False)
        in_proj_pass(xown, 0, 9, True)
        in_proj_pass(xown, 9, 8, True)

        if stop_after >= 2:
            new_phase()
            kTc = [AR.alloc([128, 16, 128], BF16) for _ in range(2)]
            vch = [AR.alloc([128, 2048], BF16) for _ in range(2)]
            qTc = [AR.alloc([128, 16, 128], BF16) for _ in range(2)]
            rgc = [AR.alloc([128, 2048], BF16) for _ in range(2)]
            Tk2, Tv2, Tq2, Tg2 = ([T("k0"), T("k1")], [T("v0"), T("v1")], [T("q0"), T("q1")], [T("g0"), T("g1")])
            S32 = AR.alloc([128, 16, 256], F32)
            Sbf = AR.alloc([128, 16, 256], BF16)
            TS32 = [T(f"S32_{h}") for h in range(RH)]
            TSbf = [T(f"Sbf_{h}") for h in range(RH)]
            kd = [AR.alloc([128, 256], BF16) for _ in range(2)]
            Tkd = [T("kd0"), T("kd1")]
            sTm = [AR.alloc([128, 128], BF16) for _ in range(2)]
            TsTm = [T("sT0"), T("sT1")]
            yall = AR.alloc([128, RH, 256], F32)
            Tyall = T("yall")
            junk2 = AR.alloc([128, 256], F32)
            Tjunk2 = T("junk2")
            ssq = AR.alloc([128, 3 * RH], F32)
            Tssq = T("ssq")
            gnw = AR.alloc([128, 2048], F32)
            dmt = AR.alloc([128, RH, 128], F32)
            Tct2 = T("ct2")
            load(gnw, gnw_d.partition_broadcast(128), [Tin], [Tct2], Tct2)
            load(dmt.rearrange("p a b -> p (a b)"), dm_d, [Tin], [Tct2], Tct2)
            ymix = AR.alloc([128, 2048], BF16)
            Tymix = T("ymix")
            mst = [AR.alloc([128, 16, 128], BF16) for _ in range(2)]
            Tmst = [T("mst0"), T("mst1")]
            for h in range(RH):
                mset("dve", S32[:, 2 * h:2 * h + 2, :], 0.0, [TS32[h]])
                mset("dve", Sbf[:, 2 * h:2 * h + 2, :], 0.0, [TSbf[h]])

            def p2_loads(lc):
                i = lc % 2
                load_split(kTc[i], rkT[:, :, :, lc * 128:(lc + 1) * 128].rearrange("h a p t -> p (h a) t"),
                           [TD("rkT")], [Tk2[i]], Tk2[i])
                load(vch[i], rv[lc * 128:(lc + 1) * 128, :], [TD("rv")], [Tv2[i]], Tv2[i])
                if lc >= NKV:
                    m_ = lc - NKV
                    load_split(qTc[i], rqT[:, :, :, m_ * 128:(m_ + 1) * 128].rearrange("h a p t -> p (h a) t"),
                               [TD("rqT")], [Tq2[i]], Tq2[i])
                    load(rgc[i], rgs[m_ * 128:(m_ + 1) * 128, :], [TD("rgs")], [Tg2[i]], Tg2[i])

            def p2_s1(lc, h, u):
                i = lc % 2
                own = lc >= NKV
                last = (lc == NL - 1)
                if not last:
                    pv = psb(u)
                    for half in range(2):
                        tr(pv[:, half * 128:(half + 1) * 128], kTc[i][:, 2 * h + half, :], identb,
                           [Tk2[i], Tidb], [Tps[u]], inc=(half == 1))
                    tsc("dve", kd[u], pv[:, 0:256], kdec[:, h:h + 1], ALU.mult, [Tps[u], Tc], [Tkd[u]])
                if own:
                    bs = 2 + u
                    for half in range(2):
                        mm(ps[:, bs, 0:128], kTc[i][:, 2 * h + half, :], qTc[i][:, 2 * h + half, :],
                           half == 0, half == 1, [Tk2[i], Tq2[i]], [Tps[bs]], inc=(half == 1))
                    tt("dve", sTm[u], ps[:, bs, 0:128], dmt[:, h, :], ALU.mult, [Tps[bs], Tct2], [TsTm[u]])

            def p2_s2(lc, h, u):
                i = lc % 2
                own = lc >= NKV
                last = (lc == NL - 1)
                if own:
                    bo = 4 + u
                    mm(ps[:, bo, 0:256], sTm[u], vch[i][:, h * 256:(h + 1) * 256], True, False,
                       [TsTm[u], Tv2[i]], [Tps[bo]], inc=False)
                    for half in range(2):
                        mm(ps[:, bo, 0:256], qTc[i][:, 2 * h + half, :], Sbf[:, 2 * h + half, :], False, half == 1,
                           [Tq2[i], TSbf[h]], [Tps[bo]], inc=(half == 1))
                    act(yall[:, h, :], ps[:, bo, 0:256], AF.Copy, [Tps[bo]], [Tyall])
                    act(junk2, ps[:, bo, 0:256], AF.Square, [Tps[bo], Tssq], [Tjunk2, Tssq], accum_out=ssq[:, h:h + 1])
                if not last:
                    bu = 6 + u
                    for half in range(2):
                        mm(ps[:, bu, half * 256:(half + 1) * 256], kd[u][:, half * 128:(half + 1) * 128],
                           vch[i][:, h * 256:(h + 1) * 256], True, True, [Tkd[u], Tv2[i]], [Tps[bu]], inc=(half == 1))
                    s32v = S32[:, 2 * h:2 * h + 2, :].rearrange("p a b -> p (a b)")
                    stt("dve", s32v, s32v, float(gam[h] ** 128), ps[:, bu, :], ALU.mult, ALU.add,
                        [Tps[bu], TS32[h]], [TS32[h]])
                    cp("act", Sbf[:, 2 * h:2 * h + 2, :].rearrange("p a b -> p (a b)"), s32v, [TS32[h]], [TSbf[h]])

            ui = 0
            pend = None
            p2_loads(0)
            for lc in range(NL):
                i = lc % 2
                own = lc >= NKV
                m = lc - NKV
                if own:
                    mset("dve", ssq[:, 0:RH], 0.0, [Tssq])
                for h in range(RH):
                    u = ui % 2
                    ui += 1
                    p2_s1(lc, h, u)
                    if pend is not None:
                        p2_s2(*pend)
                    pend = (lc, h, u)
                    if h == 0 and lc + 1 < NL:
                        p2_loads(lc + 1)
                if own:
                    p2_s2(*pend)
                    pend = None
                    rstd_from_ssq(ssq[:, 0:RH], ssq[:, RH:2 * RH], ssq[:, 2 * RH:3 * RH], 256, Tssq)
                    tt("dve", yall, yall, ssq[:, 2 * RH:3 * RH].unsqueeze(2).to_broadcast([128, RH, 256]), ALU.mult,
                       [Tyall, Tssq], [Tyall])
                    y2 = yall.rearrange("p a b -> p (a b)")
                    tt("dve", y2, y2, gnw, ALU.mult, [Tyall, Tct2], [Tyall])
                    tt("dve", ymix, y2, rgc[i], ALU.mult, [Tyall, Tg2[i]], [Tymix])
                    mi = m % 2
                    for g in range(2):
                        pv = psb(g)
                        for j in range(8):
                            c = g * 8 + j
                            tr(pv[:, j * 128:(j + 1) * 128], ymix[:, c * 128:(c + 1) * 128], identb,
                               [Tymix, Tidb], [Tps[g]], inc=(j == 7))
                        cp("act", mst[mi][:, g * 8:(g + 1) * 8, :], pv.rearrange("p (j t) -> p j t", j=8),
                           [Tps[g]], [Tmst[mi]])
                    load_split(mixT[0:16, :, m * 128:(m + 1) * 128].rearrange("c p t -> p c t"), mst[mi],
                               [Tmst[mi]], [TD("mixT")], Tmst[mi])
            if pend is not None:
                p2_s2(*pend)

        if stop_after >= 3:
            new_phase()
            kTh = [AR.alloc([128, TL], BF16) for _ in range(2)]
            vh = [AR.alloc([128, NL, 128], BF16) for _ in range(2)]
            qTh = [AR.alloc([128, TOWN], BF16) for _ in range(2)]
            Tk3, Tv3, Tq3 = [T("k30"), T("k31")], [T("v30"), T("v31")], [T("q30"), T("q31")]
            NWS = 4
            NW = 2 * NWS
            e_b = [AR.alloc([128, 512], F32) for _ in range(NW)]
            sp_b = [AR.alloc([128, 512], F32) for _ in range(NW)]
            spm_b = [AR.alloc([128, 512], BF16) for _ in range(NW)]
            Sb_b = [AR.alloc([128, 512], BF16) for _ in range(NW)]
            w_b = [AR.alloc([128, 512], BF16) for _ in range(NW)]
            Te, Tsp, Tspm, TSb, Tw = ([T(f"e{i}") for i in range(NW)], [T(f"sp{i}") for i in range(NW)],
                                      [T(f"spm{i}") for i in range(NW)], [T(f"Sb{i}") for i in range(NW)],
                                      [T(f"w{i}") for i in range(NW)])
            negtri = AR.alloc([128, 128], BF16)
            negones = AR.alloc([128, 128], BF16)
            cmaskb = AR.alloc([128, 128], BF16)
            ones32 = AR.alloc([128, 128], F32)
            tmpc = AR.alloc([128, 2, 128], F32)
            Tct3 = T("ct3")
            Ttmpc = T("tmpc")
            load(tmpc[:, 0, :], negtri_d, [Tin], [Ttmpc], Ttmpc)
            load(tmpc[:, 1, :], cmask_d, [Tin], [Ttmpc], Ttmpc)
            cp("dve", negtri, tmpc[:, 0, :], [Ttmpc], [Tct3])
            cp("dve", cmaskb, tmpc[:, 1, :], [Ttmpc], [Tct3])
            mset("dve", negones, -1.0, [Tct3])
            mset("dve", ones32, 1.0, [Tct3])
            sq3 = [AR.alloc([128, 512], F32) for _ in range(2)]
            rr3 = [AR.alloc([128, 512], F32) for _ in range(2)]
            y3 = [AR.alloc([128, 512], BF16) for _ in range(2)]
            Tsq3, Trr3, Ty3 = [T("sq30"), T("sq31")], [T("rr30"), T("rr31")], [T("y30"), T("y31")]
            groups = [[0], [1, 2, 3, 4], [5, 6, 7, 8], [9, 10, 11, 12], [13, 14, 15, 16]]

            def p3_loads(h):
                i = h % 2
                load(kTh[i], skT[h], [TD("skT")], [Tk3[i]], Tk3[i])
                load_split(vh[i], sv[:, h * 128:(h + 1) * 128].rearrange("(c p) d -> p c d", p=128), [TD("sv")], [Tv3[i]], Tv3[i])
                load(qTh[i], sqT[h], [TD("sqT")], [Tq3[i]], Tq3[i])

            NS = 2
            L2, L3 = 2, 4

            def make_group(h, i, ms, sidx):
                bA, bB, bC, bD = sidx, 2 + sidx, 4 + sidx, 6 + sidx
                nq = len(ms) * 128
                q0 = ms[0] * 128
                ctop = NKV + ms[-1]
                info = []
                for k, c in enumerate(range(ctop, -1, -1)):
                    nbelow = sum(1 for m_ in ms if NKV + m_ < c)
                    lo = nbelow * 128
                    diag = (c >= NKV) and ((c - NKV) in ms)
                    clo = lo + (128 if diag else 0)
                    info.append(dict(c=c, lo=lo, diag=diag, clo=clo, wi=sidx * NWS + k % NWS, k=k))
                n_it = len(info)

                def st1(d):
                    c, lo, wi = d["c"], d["lo"], d["wi"]
                    mm(ps[:, bA, lo:nq], kTh[i][:, c * 128:(c + 1) * 128], qTh[i][:, q0 + lo:q0 + nq], True, True,
                       [Tk3[i], Tq3[i]], [Tps[bA]], inc=True)
                    act(e_b[wi][:, lo:nq], ps[:, bA, lo:nq], AF.Exp, [Tps[bA]], [Te[wi]])

                def st1b(d):
                    c, lo, wi = d["c"], d["lo"], d["wi"]
                    act(sp_b[wi][:, lo:nq], e_b[wi][:, lo:nq], AF.Ln, [Te[wi]], [Tsp[wi]], bias=1.0)
                    tsc("dve", spm_b[wi][:, lo:nq], sp_b[wi][:, lo:nq], kval[:, c:c + 1], ALU.mult,
                        [Tsp[wi], Tc], [Tspm[wi]])
                    if d["diag"]:
                        tt("dve", spm_b[wi][:, lo:lo + 128], spm_b[wi][:, lo:lo + 128], cmaskb, ALU.mult,
                           [Tspm[wi], Tct3], [Tspm[wi]])

                def st2(d):
                    c, lo, wi, clo, k = d["c"], d["lo"], d["wi"], d["clo"], d["k"]
                    mm(ps[:, bB, lo:nq], negtri, spm_b[wi][:, lo:nq], True, False, [Tct3, Tspm[wi]], [Tps[bB]],
                       inc=False, skip_group_check=True)
                    if clo < nq:
                        pw = info[k - 1]["wi"]
                        mm(ps[:, bB, clo:nq], identb, Sb_b[pw][:, clo:nq], False, False, [Tidb, TSb[pw]], [Tps[bB]],
                           inc=False, skip_group_check=True)
                    mm(ps[:, bB, lo:nq], kTh[i][:, c * 128:(c + 1) * 128], qTh[i][:, q0 + lo:q0 + nq], False, True,
                       [Tk3[i], Tq3[i]], [Tps[bB]], inc=True, skip_group_check=True)
                    if c > 0:
                        mm(ps[:, bD, lo:nq], negones, spm_b[wi][:, lo:nq], k == 0, True, [Tct3, Tspm[wi]], [Tps[bD]],
                           inc=True, skip_group_check=True)
                        cp("dve", Sb_b[wi][:, lo:nq], ps[:, bD, lo:nq], [Tps[bD]], [TSb[wi]])
                    act(w_b[wi][:, lo:nq], ps[:, bB, lo:nq], AF.Exp, [Tps[bB]], [Tw[wi]])
                    if d["diag"]:
                        tt("dve", w_b[wi][:, lo:lo + 128], w_b[wi][:, lo:lo + 128], cmaskb, ALU.mult,
                           [Tw[wi], Tct3], [Tw[wi]])

                def st3(d):
                    c, lo, wi, k = d["c"], d["lo"], d["wi"], d["k"]
                    mm(ps[:, bC, lo:nq], vh[i][:, c, :], w_b[wi][:, lo:nq], k == 0, c == 0, [Tv3[i], Tw[wi]], [Tps[bC]],
                       inc=True, skip_group_check=True)

                def step(t):
                    if t < n_it:
                        st1(info[t])
                    if 0 <= t - L2 < n_it:
                        st2(info[t - L2])
                    if t < n_it:
                        st1b(info[t])
                    if 0 <= t - L3 < n_it:
                        st3(info[t - L3])

                def finish():
                    g2 = sidx
                    act(sq3[g2][:, 0:nq], ps[:, bC, 0:nq], AF.Square, [Tps[bC]], [Tsq3[g2]])
                    mm(ps[:, bD, 0:nq], ones32, sq3[g2][:, 0:nq], True, True, [Tct3, Tsq3[g2]], [Tps[bD]], inc=True)
                    act(sq3[g2][:, 0:nq], ps[:, bD, 0:nq], AF.Sqrt, [Tps[bD]], [Tsq3[g2]], scale=1.0 / 128, bias=EPS)
                    recip(rr3[g2][:, 0:nq], sq3[g2][:, 0:nq], [Tsq3[g2]], [Trr3[g2]])
                    stt("dve", y3[g2][:, 0:nq], ps[:, bC, 0:nq], sbw[:, h:h + 1], rr3[g2][:, 0:nq], ALU.mult, ALU.mult,
                        [Tps[bC], Tc, Trr3[g2]], [Ty3[g2]])
                    load(mixT[16 + h, :, q0:q0 + nq], y3[g2][:, 0:nq], [Ty3[g2]], [TD("mixT")], Ty3[g2])

                return n_it + L3, step, finish

            rounds = [[groups[1], groups[2]], [groups[3], groups[4]]]
            p3_loads(0)
            for h in range(SH):
                i = h % 2
                if h + 1 < SH:
                    p3_loads(h + 1)
                for rnd in [[groups[0]]] + rounds:
                    gs = [make_group(h, i, ms, sidx) for sidx, ms in enumerate(rnd)]
                    for t in range(max(g[0] for g in gs)):
                        for g in gs:
                            if t < g[0]:
                                g[1](t)
                    for g in gs:
                        g[2]()

        if stop_after >= 4:
            new_phase()
            mxp = AR.alloc([128, DC, 512], BF16)
            Tmxp = T("mxp")
            wp = [AR.alloc([128, 8, 512], BF16) for _ in range(3)]
            Twp = [T(f"wp{i}") for i in range(3)]
            stg = [AR.alloc([128, 512], F32) for _ in range(4)]
            Tstg = [T(f"stg{i}") for i in range(4)]
            NB5 = 2
            at = [AR.alloc([128, D], F32) for _ in range(NB5)]
            xt5 = [AR.alloc([128, D], F32) for _ in range(NB5)]
            xn5 = [AR.alloc([128, D], BF16) for _ in range(NB5)]
            Tat = [T(f"at{i}") for i in range(NB5)]
            Txt5 = [T(f"x5{i}") for i in range(NB5)]
            Txn5 = [T(f"n5{i}") for i in range(NB5)]
            wpb = AR.alloc([128, D], F32)
            Twpb = T("wpb")
            load(wpb, wpost_d.partition_broadcast(128), [Tin], [Twpb], Twpb)
            st5 = [AR.alloc([128, 8], F32) for _ in range(NB5)]
            Tst5 = [T(f"s5{i}") for i in range(NB5)]
            h2s = [AR.alloc([128, DC, 128], BF16) for _ in range(2)]
            Th2s = [T("h2s0"), T("h2s1")]

            def p5_a(m):
                i = m % NB5
                r0 = m * 128
                load(at[i], a_raw[r0:r0 + 128, :], [TD("a_raw", m)], [Tat[i]], Tat[i])
                load(xt5[i], xown[r0:r0 + 128, :], [Tin], [Txt5[i]], Txt5[i])
                mset("dve", st5[i][:, 0:1], 0.0, [Tst5[i]])
                act(xn5[i], at[i], AF.Square, [Tat[i], Tst5[i]], [Txn5[i], Tst5[i]], accum_out=st5[i][:, 0:1])
                rstd_from_ssq(st5[i][:, 0:1], st5[i][:, 1:2], st5[i][:, 2:3], D, Tst5[i])
                stt("dve", at[i], at[i], st5[i][:, 2:3], wpb, ALU.mult, ALU.mult, [Tat[i], Tst5[i], Twpb], [Tat[i]])
                tt("dve", xt5[i], xt5[i], at[i], ALU.add, [Txt5[i], Tat[i]], [Txt5[i]])
                load(h1_d[r0:r0 + 128, :], xt5[i], [Txt5[i]], [TD("h1", m)], Txt5[i])
                norm_part(xt5[i], xn5[i], st5[i][:, 4:8], Txt5[i], Txn5[i], Tst5[i])

            def p5_b(m):
                i = m % NB5
                hi = m % 2
                r0 = m * 128
                transpose_part(xn5[i], Txn5[i], wffnpre, lambda c0, hi=hi: h2s[hi][:, c0:c0 + 8, :], [], [Th2s[hi]], (4, 5))
                load_split(h2T[:, :, r0:r0 + 128].rearrange("c p t -> p c t"), h2s[hi], [Th2s[hi]], [TD("h2T")], Th2s[hi])

            do_p5 = stop_after >= 5
            passes = [(0, 1)] + [(1 + 4 * p, 4) for p in range(4)]
            wq4 = 0
            sq4 = 0
            prev_chunks = []
            for (c0, nch) in passes:
                n = nch * 128
                load_split(mxp[:, :, 0:n], mixT[:, :, c0 * 128:c0 * 128 + n].rearrange("c p t -> p c t"),
                           [TD("mixT")], [Tmxp], Tmxp)
                todo = list(prev_chunks) if do_p5 else []
                pend_b = None
                for cb in range(8):
                    for kg in range(4):
                        s = wq4 % 3
                        wq4 += 1
                        wload(wp[s].rearrange("p a b -> p (a b)"), wout_d[cb][:, kg * 8 * 512:(kg + 1) * 8 * 512], Twp[s])
                        for tl in range(nch):
                            for kc in range(8):
                                k = kg * 8 + kc
                                mm(ps[:, tl, :], mxp[:, k, tl * 128:(tl + 1) * 128], wp[s][:, kc, :],
                                   k == 0, k == DC - 1, [Tmxp, Twp[s]], [Tps[tl]], inc=(kc == 7))
                    if pend_b is not None:
                        p5_b(pend_b)
                        pend_b = None
                    for tl in range(nch):
                        si = sq4 % 4
                        sq4 += 1
                        if tl % 2 == 0:
                            act(stg[si], ps[:, tl, :], AF.Copy, [Tps[tl]], [Tstg[si]])
                        else:
                            cp("dve", stg[si], ps[:, tl, :], [Tps[tl]], [Tstg[si]])
                        r0 = (c0 + tl) * 128
                        load(a_raw[r0:r0 + 128, cb * 512:(cb + 1) * 512], stg[si], [Tstg[si]], [TD("a_raw", c0 + tl)], Tstg[si])
                    if cb % 2 == 0 and todo:
                        m_ = todo.pop(0)
                        p5_a(m_)
                        pend_b = m_
                if pend_b is not None:
                    p5_b(pend_b)
                assert not todo
                prev_chunks = list(range(c0, c0 + nch))
            if do_p5:
                for m_ in prev_chunks:
                    p5_a(m_)
                    p5_b(m_)

        if stop_after >= 6:
            new_phase()
            h2p = AR.alloc([128, DC, 512], BF16)
            Th2p = T("h2p")
            actT = AR.alloc([128, FB, 512], BF16)
            TactT = T("actT")
            wreg = AR.alloc([128, 4, 4096], BF16)
            Twr = [T(f"wr{i}") for i in range(4)]
            graw = [AR.alloc([128, 514], F32) for _ in range(2)]
            tcv = [AR.alloc([128, 512], F32) for _ in range(2)]
            scv = [AR.alloc([128, 512], F32) for _ in range(2)]
            Tgraw, Ttcv, Tscv = [T("gr0"), T("gr1")], [T("tc0"), T("tc1")], [T("sc0"), T("sc1")]
            halo = AR.alloc([128, 2, FB, 2], F32)
            Thalo = [T("halo0"), T("halo1")]
            cw = AR.alloc([128, FB, 3], F32)
            cbv = AR.alloc([128, FB], F32)
            Tcv = T("convw")
            load(cw.rearrange("p a b -> p (a b)"), cw_d, [Tin], [Tcv], Tcv)
            load(cbv, cb_d, [Tin], [Tcv], Tcv)
            stg6 = [AR.alloc([128, 512], F32) for _ in range(4)]
            Tstg6 = [T(f"sg{i}") for i in range(4)]
            h2h = AR.alloc([128, DC, 128], BF16)
            Th2h = T("h2h")
            load_split(h2h, h2T[:, :, 0:128].rearrange("c p t -> p c t"), [TD("h2T")], [Th2h], Th2h)
            wq6 = 0
            pq = 0
            sq6 = 0
            for p in range(4):
                hin, hout = p % 2, (p + 1) % 2
                c0 = 1 + 4 * p
                load_split(h2p, h2T[:, :, c0 * 128:c0 * 128 + 512].rearrange("c p t -> p c t"), [TD("h2T")], [Th2p], Th2p)
                for fb in range(FB):
                    s2 = (pq % 2) * 2
                    u = pq % 2
                    pq += 1
                    wload(wreg[:, s2, :], wup_d[fb], Twr[s2])
                    wload(wreg[:, s2 + 1, :], wup_d[FB + fb], Twr[s2 + 1])
                    bG, bU = 2 * u, 2 * u + 1
                    for (sl, bk) in ((s2, bG), (s2 + 1, bU)):
                        w3 = wreg[:, sl, :].rearrange("p (k c) -> p k c", k=DC)
                        for kc in range(DC):
                            mm(ps[:, bk, :], w3[:, kc, :], h2p[:, kc, :], kc == 0, kc == DC - 1,
                               [Twr[sl], Th2p], [Tps[bk]], inc=(kc == DC - 1))
                    g = graw[u]
                    if p == 0:
                        hb = 4 + u
                        w3g = wreg[:, s2, :].rearrange("p (k c) -> p k c", k=DC)
                        for kc in range(DC):
                            mm(ps[:, hb, 0:2], w3g[:, kc, :], h2h[:, kc, 126:128], kc == 0, kc == DC - 1,
                               [Twr[s2], Th2h], [Tps[hb]], inc=(kc == DC - 1))
                    act(g[:, 2:514], ps[:, bG, :], AF.Copy, [Tps[bG]], [Tgraw[u]])
                    if p == 0:
                        act(g[:, 0:2], ps[:, hb, 0:2], AF.Copy, [Tps[hb]], [Tgraw[u]])
                    else:
                        cp("dve", g[:, 0:2], halo[:, hin, fb, :], [Thalo[hin]], [Tgraw[u]])
                    cp("dve", halo[:, hout, fb, :], g[:, 512:514], [Tgraw[u]], [Thalo[hout]])
                    tsc("dve", tcv[u], g[:, 2:514], cw[:, fb, 2:3], ALU.mult, [Tgraw[u], Tcv], [Ttcv[u]],
                        s2=cbv[:, fb:fb + 1], op1=ALU.add)
                    stt("dve", tcv[u], g[:, 1:513], cw[:, fb, 1:2], tcv[u], ALU.mult, ALU.add, [Tgraw[u], Tcv, Ttcv[u]], [Ttcv[u]])
                    stt("dve", tcv[u], g[:, 0:512], cw[:, fb, 0:1], tcv[u], ALU.mult, ALU.add, [Tgraw[u], Tcv, Ttcv[u]], [Ttcv[u]])
                    act(scv[u], tcv[u], AF.Silu, [Ttcv[u]], [Tscv[u]])
                    tt("dve", actT[:, fb, :], scv[u], ps[:, bU, :], ALU.mult, [Tscv[u], Tps[bU]], [TactT])
                pieces = [(k0, min(8, FB - k0)) for k0 in range(0, FB, 8)]
                for cb in range(8):
                    for (k0, nk) in pieces:
                        s = wq6 % 4
                        wq6 += 1
                        wload(wreg[:, s, 0:nk * 512], wdn_d[cb][:, k0 * 512:(k0 + nk) * 512], Twr[s])
                        w3 = wreg[:, s, :].rearrange("p (k c) -> p k c", k=8)
                        for tl in range(4):
                            for kc in range(nk):
                                k = k0 + kc
                                mm(ps[:, 4 + tl, :], actT[:, k, tl * 128:(tl + 1) * 128], w3[:, kc, :],
                                   k == 0, k == FB - 1, [TactT, Twr[s]], [Tps[4 + tl]], inc=(kc == nk - 1))
                    for tl in range(4):
                        si = sq6 % 4
                        sq6 += 1
                        if tl % 2 == 0:
                            act(stg6[si], ps[:, 4 + tl, :], AF.Copy, [Tps[4 + tl]], [Tstg6[si]])
                        else:
                            cp("dve", stg6[si], ps[:, 4 + tl, :], [Tps[4 + tl]], [Tstg6[si]])
                        ch = 4 * p + tl
                        load(f_raw[ch * 128:(ch + 1) * 128, cb * 512:(cb + 1) * 512], stg6[si], [Tstg6[si]],
                             [TD("f_raw", ch)], Tstg6[si])

        outs = []
        if stop_after >= 7:
            new_phase()
            NB7 = 3
            ft = [AR.alloc([128, D], F32) for _ in range(NB7)]
            ht = [AR.alloc([128, D], F32) for _ in range(NB7)]
            jk = AR.alloc([128, D], BF16)
            Tft = [T(f"ft{i}") for i in range(NB7)]
            Tht = [T(f"ht{i}") for i in range(NB7)]
            Tjk = T("jk")
            wfb = AR.alloc([128, D], F32)
            Twfb = T("wfb")
            load(wfb, wffnpost_d.partition_broadcast(128), [Tin], [Twfb], Twfb)
            st7 = [AR.alloc([128, 4], F32) for _ in range(NB7)]
            Tst7 = [T(f"s7{i}") for i in range(NB7)]

            def p7_loads(ch):
                i = ch % NB7
                load(ft[i], f_raw[ch * 128:(ch + 1) * 128, :], [TD("f_raw", ch)], [Tft[i]], Tft[i])
                load(ht[i], h1_d[(ch + 1) * 128:(ch + 2) * 128, :], [TD("h1", ch + 1)], [Tht[i]], Tht[i])

            p7_loads(0)
            p7_loads(1)
            for ch in range(16):
                i = ch % NB7
                if ch + 2 < 16:
                    p7_loads(ch + 2)
                mset("dve", st7[i][:, 0:1], 0.0, [Tst7[i]])
                act(jk, ft[i], AF.Square, [Tft[i], Tst7[i]], [Tjk, Tst7[i]], accum_out=st7[i][:, 0:1])
                rstd_from_ssq(st7[i][:, 0:1], st7[i][:, 1:2], st7[i][:, 2:3], D, Tst7[i])
                stt("dve", ft[i], ft[i], st7[i][:, 2:3], wfb, ALU.mult, ALU.mult, [Tft[i], Tst7[i], Twfb], [Tft[i]])
                tt("dve", ht[i], ht[i], ft[i], ALU.add, [Tht[i], Tft[i]], [Tht[i]])
                outs.append(load(out_d[ch * 128:(ch + 1) * 128, :], ht[i], [Tht[i]], [TD("out", ch)], Tht[i], q="pool"))

        S.barrier()
        print(f"[kernel] instructions={S.n_ins} waits={S.n_wait} sems={S.nsem}")
        S.run_block()
    return nc


def _consts():
    ident = np.eye(128, dtype=np.float32)
    k = np.arange(128)
    negtri = np.where(k[:, None] >= k[None, :], -1.0, 0.0).astype(np.float32)
    cmask = np.where(k[:, None] < k[None, :], 1.0, 0.0).astype(np.float32)
    gam = (1.0 - 2.0 ** (-5.0 - np.arange(RH, dtype=np.float64)))
    n = np.arange(128, dtype=np.float64)
    gq = (256.0 ** -0.5) * gam[:, None] ** (n[None, :] + 1.0)
    gq = np.broadcast_to(gq.reshape(1, RH * 128), (128, RH * 128)).astype(np.float32)
    dm = np.zeros((128, RH, 128), np.float64)
    for h in range(RH):
        dm[:, h, :] = np.where(n[None, :] >= n[:, None], gam[h] ** (-(n[:, None] + 1.0)), 0.0)
    dm = dm.reshape(128, RH * 128).astype(np.float32)
    kdec = (gam[None, :] ** (127.0 - n[:, None])).astype(np.float32)
    return ident, negtri, cmask, gq, dm, kdec


def _rope_tables(pos):
    half = 128
    inv = (10000.0 ** (-np.arange(half, dtype=np.float32) / half)).astype(np.float32)
    ang = pos.astype(np.float32)[None, :] * inv[:, None]
    return np.stack([np.cos(ang), np.sin(ang)]).astype(np.float32)


def _prep_shared(inputs):
    f = lambda a: np.ascontiguousarray(np.asarray(a, dtype=np.float32))
    w_in = f(inputs["w_in"])[0]
    w_out = f(inputs["w_out"])[0]
    w_up = f(inputs["w_up"])[0]
    w_down = f(inputs["w_down"])[0]
    sh = {}
    sh["win"] = np.ascontiguousarray(w_in.reshape(DC, 128, 112, 128).transpose(2, 1, 0, 3)).reshape(112, 128, DC * 128)
    sh["wout"] = np.ascontiguousarray(w_out.reshape(DC, 128, 8, 512).transpose(2, 1, 0, 3)).reshape(8, 128, DC * 512)
    sh["wup"] = np.ascontiguousarray(w_up.reshape(DC, 128, 2 * FB, 128).transpose(2, 1, 0, 3)).reshape(2 * FB, 128, DC * 128)
    sh["wdn"] = np.ascontiguousarray(w_down.reshape(FB, 128, 8, 512).transpose(2, 1, 0, 3)).reshape(8, 128, FB * 512)
    colmaj = lambda v, n: np.ascontiguousarray(f(v).reshape(n, 128).T)
    sh["wpre"] = colmaj(inputs["attn_pre_norm_w"][0], DC)
    sh["wffnpre"] = colmaj(inputs["ffn_pre_norm_w"][0], DC)
    sh["wpost"] = f(inputs["attn_post_norm_w"]).reshape(1, D)
    sh["wffnpost"] = f(inputs["ffn_post_norm_w"]).reshape(1, D)
    sh["gnw"] = f(inputs["ret_gn_w"]).reshape(1, 2048)
    sh["sbw"] = colmaj(inputs["sb_norm_w"][0], SH)
    cwv = f(inputs["conv_w"])[0]
    sh["cw"] = np.ascontiguousarray(cwv.reshape(3, FB, 128).transpose(2, 1, 0)).reshape(128, FB * 3)
    sh["cb"] = colmaj(inputs["conv_b"][0], FB)
    ident, negtri, cmask, gq, dm, kdec = _consts()
    sh.update(ident=ident, negtri=negtri, cmask=cmask, gq=gq, dm=dm, kdec=kdec)
    return sh


def _prep_core(inputs, b, j):
    x = np.asarray(inputs["x"], dtype=np.float32)
    meta = np.asarray(inputs["meta_tokens"], dtype=np.float32)
    metachunk = np.zeros((128, D), np.float32)
    metachunk[112:] = meta
    kval = np.ones((128, NL), np.float32)
    if j == 0:
        xown = np.concatenate([metachunk, x[b, 0:2048]], axis=0)
        xkv = np.zeros((TKV, D), np.float32)
        kval[:, 0:NKV] = 0.0
        kval[0:112, NKV] = 0.0
        own_pc0 = 0
        kv_pos = np.zeros(TKV, np.float32)
    else:
        xown = np.ascontiguousarray(x[b, 15 * 128:4096])
        xkv = np.concatenate([metachunk, x[b, 0:15 * 128]], axis=0)
        kval[0:112, 0] = 0.0
        own_pc0 = 16
        kv_pos = np.arange(TKV, dtype=np.float32) - 112.0
    own_pos = np.arange(TOWN, dtype=np.float32) + own_pc0 * 128 - 112.0
    return {
        "xown": np.ascontiguousarray(xown), "xkv": np.ascontiguousarray(xkv),
        "cs_own": _rope_tables(own_pos), "cs_kv": _rope_tables(kv_pos), "kval": kval,
    }


_CACHE = {}


def kernel(**inputs):
    stop_after = int(os.environ.get("MK_STOP", "99"))
    debug = bool(int(os.environ.get("MK_DEBUG", "0")))
    key = (stop_after, debug)
    if key not in _CACHE:
        _CACHE[key] = build_program(stop_after, debug)
    nc = _CACHE[key]
    sh = _prep_shared(inputs)
    ncores = int(os.environ.get("MK_NCORES", "8"))
    in_maps = []
    for c in range(ncores):
        b, j = divmod(c, 2)
        m = dict(sh)
        m.update(_prep_core(inputs, b, j))
        in_maps.append(m)
    res = run_bass_kernel_spmd(nc, in_maps, core_ids=list(range(ncores)))
    if debug:
        kernel.last_results = res.results
    out = np.zeros((BATCH, SEQ, D), np.float32)
    for c in range(ncores):
        b, j = divmod(c, 2)
        out[b, j * 2048:(j + 1) * 2048] = np.asarray(res.results[c]["out"], dtype=np.float32)
    return out
```
